# Optimizing a Trainium2 kernel written in Bass

```python
import math
import jax, jax.numpy as jnp
from jax import lax
import numpy as np

D_MODEL = 1024
BATCH = 8
SEQ = 2048
DEPTH = 1
DEC_BATCH = 128
DEC_SEQ = 8
PAST_LEN = 16384
PAGE_SIZE = 128

D_MIX = 2 * D_MODEL
D_M = D_MIX // 2
D_H = D_MIX - D_M
M_HEADS = 4
M_DV = D_M // M_HEADS
M_DK = M_DV // 2
CONV_W = 4
H_HEADS = 8
H_DK = D_H // H_HEADS
H_DV = D_H // H_HEADS
CHUNK = 64
EPS = 1e-6
SPLITS = (D_M, D_M, D_M, D_H, D_H, D_H, D_H)
N_IN = sum(SPLITS)
GATE_IN = 2 * M_HEADS * M_DK + D_M

kernel_name = "hybrid_mlstm_hgrn2_step"


def rmsnorm(x, g):
    xf = x.astype(jnp.float32)
    y = xf * lax.rsqrt(jnp.mean(xf * xf, -1, keepdims=True) + EPS)
    return y * g.astype(jnp.float32)


def layernorm(x, g):
    xf = x.astype(jnp.float32)
    mu = jnp.mean(xf, -1, keepdims=True)
    var = jnp.mean(jnp.square(xf - mu), -1, keepdims=True)
    return (xf - mu) * lax.rsqrt(var + EPS) * g.astype(jnp.float32)


def causal_conv(x, buf, w, b):
    T = x.shape[1]
    xp = jnp.concatenate([buf.astype(x.dtype), x], axis=1)
    y = sum(xp[:, j:j + T] * w[j] for j in range(CONV_W)) + b
    return y, xp[:, -(CONV_W - 1):]


def to_chunks(a, nC, L):
    return jnp.moveaxis(a.reshape(a.shape[:2] + (nC, L) + a.shape[3:]), 2, 0)


def from_chunks(a):
    a = jnp.moveaxis(a, 0, 2)
    return a.reshape(a.shape[:2] + (a.shape[2] * a.shape[3],) + a.shape[4:])


def mlstm_scan(q, k, v, ig, lf, C0, n0, m0):
    T = q.shape[2]
    L = math.gcd(T, CHUNK)
    nC = T // L
    causal = jnp.tril(jnp.ones((L, L), bool))

    def step(carry, inp):
        C, n, m = carry
        qc, kc, vc, ic, fc = inp
        b = jnp.cumsum(fc, -1)
        logD = jnp.where(causal, b[..., :, None] - b[..., None, :] + ic[..., None, :], -jnp.inf)
        m_inter = b + m[..., None]
        m_t = jnp.maximum(jnp.max(logD, -1), m_inter)
        D = jnp.exp(logD - m_t[..., None])
        g = jnp.exp(m_inter - m_t)
        s = jnp.einsum('bhtk,bhsk->bhts', qc, kc) * D
        num = jnp.einsum('bhts,bhsv->bhtv', s, vc) + g[..., None] * jnp.einsum('bhtk,bhkv->bhtv', qc, C)
        den = jnp.sum(s, -1) + g * jnp.einsum('bhtk,bhk->bht', qc, n)
        h = num / jnp.maximum(jnp.abs(den), jnp.exp(-m_t))[..., None]
        w = D[..., -1, :]
        decay = g[..., -1]
        C_new = decay[..., None, None] * C + jnp.einsum('bhs,bhsk,bhsv->bhkv', w, kc, vc)
        n_new = decay[..., None] * n + jnp.einsum('bhs,bhsk->bhk', w, kc)
        return (C_new, n_new, m_t[..., -1]), h

    xs = (to_chunks(q, nC, L), to_chunks(k, nC, L), to_chunks(v, nC, L),
          to_chunks(ig, nC, L), to_chunks(lf, nC, L))
    (C, n, m), h = lax.scan(step, (C0, n0, m0), xs)
    return from_chunks(h), C, n, m


def hgrn2_scan(q, k, v, lf, S0):
    T = q.shape[2]
    L = math.gcd(T, CHUNK)
    nC = T // L
    causal = jnp.tril(jnp.ones((L, L), bool))[:, :, None]

    def step(S, inp):
        qc, kc, vc, fc = inp
        b = jnp.cumsum(fc, 2)
        diff = b[:, :, :, None, :] - b[:, :, None, :, :]
        D = jnp.exp(jnp.where(causal, diff, -jnp.inf))
        A = jnp.einsum('bhtk,bhsk,bhtsk->bhts', qc, kc, D)
        o = jnp.einsum('bhts,bhsv->bhtv', A, vc) + jnp.einsum('bhtk,bhkv->bhtv', qc * jnp.exp(b), S)
        kd = kc * jnp.exp(b[:, :, -1:] - b)
        S_new = jnp.exp(b[:, :, -1])[..., None] * S + jnp.einsum('bhsk,bhsv->bhkv', kd, vc)
        return S_new, o

    xs = (to_chunks(q, nC, L), to_chunks(k, nC, L), to_chunks(v, nC, L), to_chunks(lf, nC, L))
    S, o = lax.scan(step, S0, xs)
    return from_chunks(o), S


def mixer_layer(x, conv_buf, C0, n0, m0, S0, lb, g_norm, w_in, conv_w, conv_b, w_q, w_k, w_v,
                w_gate, b_gate, m_ln, m_skip, h_norm, w_out):
    B, T, _ = x.shape
    dt = x.dtype
    f32 = jnp.float32
    xn = rmsnorm(x, g_norm).astype(dt)
    proj = xn @ w_in
    offs = np.cumsum((0,) + SPLITS)
    xm, zm, om, fh, qh, ih, zh = [proj[..., offs[i]:offs[i + 1]] for i in range(len(SPLITS))]

    xc, conv_new = causal_conv(xm, conv_buf, conv_w, conv_b)
    xc = jax.nn.silu(xc)
    xc_h = xc.reshape(B, T, M_HEADS, M_DV)
    xm_h = xm.reshape(B, T, M_HEADS, M_DV)
    q = jnp.einsum('bthd,hdk->bthk', xc_h, w_q)
    k = jnp.einsum('bthd,hdk->bthk', xc_h, w_k)
    v = jnp.einsum('bthd,hdv->bthv', xm_h, w_v)
    gate_in = jnp.concatenate([q.reshape(B, T, -1), k.reshape(B, T, -1), v.reshape(B, T, -1)], -1)
    gates = (gate_in @ w_gate + b_gate).astype(f32)
    ig = jnp.transpose(gates[..., :M_HEADS], (0, 2, 1))
    lf = jnp.transpose(jax.nn.log_sigmoid(gates[..., M_HEADS:]), (0, 2, 1))
    qf = jnp.transpose(q, (0, 2, 1, 3)).astype(f32) * (M_DK ** -0.5)
    kf = jnp.transpose(k, (0, 2, 1, 3)).astype(f32)
    vf = jnp.transpose(v, (0, 2, 1, 3)).astype(f32)
    hm, C1, n1, m1 = mlstm_scan(qf, kf, vf, ig, lf, C0.astype(f32), n0.astype(f32), m0.astype(f32))
    hm = layernorm(jnp.transpose(hm, (0, 2, 1, 3)), m_ln).reshape(B, T, D_M)
    hm = hm * jax.nn.sigmoid(om.astype(f32))
    hm = (hm + m_skip * xc.astype(f32)) * jax.nn.silu(zm.astype(f32))

    fl = fh.astype(f32)
    lbf = lb.astype(f32)
    logf = jnp.log(lbf + (1.0 - lbf) * jax.nn.sigmoid(fl))
    kh = (1.0 - lbf) * jax.nn.sigmoid(-fl)
    qs = jax.nn.silu(qh.astype(f32))
    heads = lambda a: jnp.transpose(a.reshape(B, T, H_HEADS, -1), (0, 2, 1, 3))
    oh, S1 = hgrn2_scan(heads(qs), heads(kh), heads(ih.astype(f32)), heads(logf), S0.astype(f32))
    oh = rmsnorm(jnp.transpose(oh, (0, 2, 1, 3)), h_norm).reshape(B, T, D_H)
    oh = oh * jax.nn.silu(zh.astype(f32))

    mix = jnp.concatenate([hm, oh], -1).astype(dt)
    y = x + mix @ w_out
    return y, conv_new, C1, n1, m1, S1


def setup_inputs(seed: int = 0) -> dict:
    key = jax.random.key(seed)
    ks = jax.random.split(key, 24)
    nrm = jax.random.normal
    f32 = jnp.float32
    b_gate = jnp.concatenate([0.1 * nrm(ks[0], (DEPTH, M_HEADS), f32),
                              3.0 + 0.1 * nrm(ks[1], (DEPTH, M_HEADS), f32)], -1)
    return {
        "x_prompt": nrm(ks[2], (BATCH, SEQ, D_MODEL), f32),
        "x_sample": nrm(ks[3], (DEC_BATCH, DEC_SEQ, D_MODEL), f32),
        "state_mlstm_conv": nrm(ks[4], (DEPTH, DEC_BATCH, CONV_W - 1, D_M), f32),
        "state_mlstm_C": nrm(ks[5], (DEPTH, DEC_BATCH, M_HEADS, M_DK, M_DV), f32),
        "state_mlstm_n": nrm(ks[6], (DEPTH, DEC_BATCH, M_HEADS, M_DK), f32),
        "state_mlstm_m": nrm(ks[7], (DEPTH, DEC_BATCH, M_HEADS), f32),
        "state_hgrn_S": nrm(ks[8], (DEPTH, DEC_BATCH, H_HEADS, H_DK, H_DV), f32),
        "g_norm": 1.0 + 0.02 * nrm(ks[9], (DEPTH, D_MODEL), f32),
        "w_in": nrm(ks[10], (DEPTH, D_MODEL, N_IN), f32) * D_MODEL ** -0.5,
        "conv_w": nrm(ks[11], (DEPTH, CONV_W, D_M), f32) * CONV_W ** -0.5,
        "conv_b": 0.02 * nrm(ks[12], (DEPTH, D_M), f32),
        "w_q": nrm(ks[13], (DEPTH, M_HEADS, M_DV, M_DK), f32) * M_DV ** -0.5,
        "w_k": nrm(ks[14], (DEPTH, M_HEADS, M_DV, M_DK), f32) * M_DV ** -0.5,
        "w_v": nrm(ks[15], (DEPTH, M_HEADS, M_DV, M_DV), f32) * M_DV ** -0.5,
        "w_gate": nrm(ks[16], (DEPTH, GATE_IN, 2 * M_HEADS), f32) * GATE_IN ** -0.5,
        "b_gate": b_gate,
        "m_ln": 1.0 + 0.02 * nrm(ks[17], (DEPTH, M_HEADS, M_DV), f32),
        "m_skip": 1.0 + 0.02 * nrm(ks[18], (DEPTH, D_M), f32),
        "lb_param": 0.5 * nrm(ks[19], (DEPTH + 1, D_H), f32),
        "h_norm": 1.0 + 0.02 * nrm(ks[20], (DEPTH, H_HEADS, H_DV), f32),
        "w_out": nrm(ks[21], (DEPTH, D_MIX, D_MODEL), f32) * D_MIX ** -0.5,
        "g_final": 1.0 + 0.02 * nrm(ks[22], (D_MODEL,), f32),
    }


def reference(x_prompt, x_sample, state_mlstm_conv, state_mlstm_C, state_mlstm_n, state_mlstm_m,
              state_hgrn_S, g_norm, w_in, conv_w, conv_b, w_q, w_k, w_v, w_gate, b_gate, m_ln,
              m_skip, lb_param, h_norm, w_out, g_final):
    f32 = jnp.float32
    lb_all = jnp.cumsum(jax.nn.softmax(lb_param.astype(f32), axis=0), axis=0)
    Bp = x_prompt.shape[0]
    zC = jnp.zeros((Bp, M_HEADS, M_DK, M_DV), f32)
    zn = jnp.zeros((Bp, M_HEADS, M_DK), f32)
    zm = jnp.zeros((Bp, M_HEADS), f32)
    zS = jnp.zeros((Bp, H_HEADS, H_DK, H_DV), f32)
    zconv = jnp.zeros((Bp, CONV_W - 1, D_M), x_prompt.dtype)
    hp, hs = x_prompt, x_sample
    p_new = [[] for _ in range(5)]
    s_new = [[] for _ in range(5)]
    for l in range(DEPTH):
        w = (lb_all[l], g_norm[l], w_in[l], conv_w[l], conv_b[l], w_q[l], w_k[l], w_v[l],
             w_gate[l], b_gate[l], m_ln[l], m_skip[l], h_norm[l], w_out[l])
        hp, *pst = mixer_layer(hp, zconv, zC, zn, zm, zS, *w)
        hs, *sst = mixer_layer(hs, state_mlstm_conv[l], state_mlstm_C[l], state_mlstm_n[l],
                               state_mlstm_m[l], state_hgrn_S[l], *w)
        for i in range(5):
            p_new[i].append(pst[i])
            s_new[i].append(sst[i])
    y_prompt = rmsnorm(hp, g_final).astype(x_prompt.dtype)
    y_sample = rmsnorm(hs, g_final).astype(x_sample.dtype)
    p_conv, p_C, p_n, p_m, p_S = [jnp.stack(a, 0) for a in p_new]
    s_conv, s_C, s_n, s_m, s_S = [jnp.stack(a, 0) for a in s_new]
    return (y_prompt, y_sample, p_conv, p_C, p_n, p_m, p_S, s_conv, s_C, s_n, s_m, s_S)
```

```python
import numpy as np
from contextlib import ExitStack
import concourse.bass as bass
import concourse.mybir as mybir
from concourse.bass_utils import run_bass_kernel_spmd

F32 = mybir.dt.float32
BF16 = mybir.dt.bfloat16
AF = mybir.ActivationFunctionType
ALU = mybir.AluOpType
AX = mybir.AxisListType
EPS = 1e-6


class _Rec:
    def __getattr__(self, name):
        def call(*a, **k):
            self.cap = (name, a, k)
            return self
        return call


def _capture(fn):
    r = _Rec()
    fn(r)
    name, a, k = r.cap
    f = lambda e: getattr(e, name)(*a, **k)
    f.desc = (name, k.get("out", a[0] if a else None))
    return f


class Prog:
    ENGS = ("pe", "act", "dve", "pool", "sp")
    CE = ("pe", "act", "dve", "pool")

    def __init__(self, nc, stack):
        self.nc = nc
        self.q = {e: [] for e in self.ENGS}
        self.n = {e: 0 for e in self.ENGS}
        self.incflag = {e: [None] for e in self.ENGS}
        self.seen = {e: {} for e in self.ENGS}
        self.lastw = {}
        self.readers = {}
        self.dsem = {}
        self.dcnt = {}
        self._stack = stack
        self.bar = {}
        self.stopped = False
        self.bar_tile = stack.enter_context(nc.sbuf_tensor("bar_tile", [128, 2], F32))
        self.esem = {e: stack.enter_context(nc.semaphore("sem_" + e)) for e in self.CE}

    def _slot(self, name):
        if name not in self.dsem:
            self.dsem[name] = self._stack.enter_context(self.nc.semaphore("dma_" + name))
            self.dcnt[name] = 0
        return self.dsem[name]

    def _prune(self, eng, tickets):
        best = {}
        for t in tickets:
            key = t[1] if t[0] == "eng" else id(t[1])
            if key not in best or best[key][2] < t[2]:
                best[key] = t
        waits = []
        for key, t in best.items():
            if self.seen[eng].get(key, 0) >= t[2]:
                continue
            self.seen[eng][key] = t[2]
            waits.append(t)
        return waits

    def _deps(self, eng, reads, writes):
        deps = []
        for k in reads:
            w = self.lastw.get(k)
            if w is not None:
                deps.append((w, "raw"))
        for k in writes:
            w = self.lastw.get(k)
            if w is not None:
                deps.append((w, "waw"))
            for r in self.readers.get(k, ()):
                deps.append((r, "war"))
        bt = self.bar.get(eng)
        if bt is not None:
            self.bar[eng] = None
            deps.append((bt, "raw"))
        tickets = []
        for (t, kind) in deps:
            if t[0] == "eng" and t[1] == eng:
                if eng == "pe" or kind != "raw":
                    continue
            tickets.append(t)
        return self._prune(eng, tickets)

    def _commit(self, ticket, reads, writes):
        for k in reads:
            self.readers.setdefault(k, []).append(ticket)
        for k in writes:
            self.lastw[k] = ticket
            self.readers[k] = []

    def _all_outstanding(self, skip_slots=()):
        ts = []
        for e in self.CE:
            if self.n[e] > 0:
                ts.append(("eng", e, self.n[e]))
        for s_, sem in self.dsem.items():
            if s_ in skip_slots:
                continue
            ts.append(("dma", sem, 16 * self.dcnt[s_]))
        return ts

    def full_barrier(self):
        if self.stopped:
            return
        ts = [t for t in self._all_outstanding(("wst0", "wst1", "pC", "pS")) if not (t[0] == "eng" and t[1] == "dve")]
        waits = self._prune("dve", ts)
        self.n["dve"] += 1
        self.incflag["dve"].append(True)
        ticket = ("eng", "dve", self.n["dve"])
        fn = _capture(lambda e: e.memset(self.bar_tile[:, 0:1], 0.0))
        self.q["dve"].append((waits, fn, ticket))
        for e in self.ENGS:
            if e != "dve":
                self.bar[e] = ticket

    def op(self, eng, fn, reads=(), writes=(), inc=True):
        if self.stopped:
            return
        fn = _capture(fn)
        waits = self._deps(eng, reads, writes)
        self.n[eng] += 1
        self.incflag[eng].append(bool(inc))
        ticket = ("eng", eng, self.n[eng])
        self.q[eng].append((waits, fn, ticket))
        self._commit(ticket, reads, writes)

    def dma(self, eng, fns, slot, reads=(), writes=()):
        if self.stopped:
            return
        if not isinstance(fns, (list, tuple)):
            fns = [fns]
        fns = [_capture(f_) for f_ in fns]
        sem = self._slot(slot)
        waits = self._deps(eng, reads, writes)
        prev = self.dcnt[slot]
        if prev > 0:
            waits += self._prune(eng, [("dma", sem, 16 * prev)])
        self.dcnt[slot] += len(fns)
        ticket = ("dma", sem, 16 * self.dcnt[slot])
        for i, fn in enumerate(fns):
            self.q[eng].append((waits if i == 0 else [], fn, ("dmainc", sem)))
        self._commit(ticket, reads, writes)

    def final_wait(self, eng="sp"):
        self.q[eng].append((self._all_outstanding(), None, None))

    def emit(self, block):
        q = self.q
        nxt = {}
        for e in self.CE:
            fl = self.incflag[e]
            r = [0] * (len(fl) + 1)
            last = None
            for j in range(len(fl) - 1, 0, -1):
                if fl[j]:
                    last = j
                r[j] = last
            nxt[e] = r
        needed = {e: set() for e in self.CE}
        for e in self.ENGS:
            for waits, fn, tk in q[e]:
                for t in waits:
                    if t[0] == "eng":
                        j = nxt[t[1]][t[2]]
                        assert j is not None, ("dependency on a trailing non-signalling instruction", t)
                        needed[t[1]].add(j)
        if EAGER_SIGNALS:
            for e in self.CE:
                needed[e] = {j for j in range(1, len(self.incflag[e])) if self.incflag[e][j]}
        rank = {}
        for e in self.CE:
            rk = {}
            for c, j in enumerate(sorted(needed[e])):
                rk[j] = c + 1
            rank[e] = rk
        esem = self.esem

        def run(e, lst):
            for waits, fn, tk in lst:
                for t in waits:
                    if t[0] == "eng":
                        e.wait_ge(esem[t[1]], rank[t[1]][nxt[t[1]][t[2]]])
                    else:
                        e.wait_ge(t[1], t[2])
                if fn is None:
                    continue
                ins = fn(e)
                if tk is None:
                    continue
                if tk[0] == "dmainc":
                    ins.then_inc(tk[1], 16)
                elif tk[2] in needed[tk[1]]:
                    ins.then_inc(esem[tk[1]], 1)

        @block.sync
        def _(e):
            run(e, q["sp"])

        @block.gpsimd
        def _(e):
            run(e, q["pool"])

        @block.scalar
        def _(e):
            run(e, q["act"])

        @block.vector
        def _(e):
            run(e, q["dve"])

        @block.tensor
        def _(e):
            run(e, q["pe"])


class _Stop(Exception):
    pass


DBG_STOP = None
DBG_PRINT = False
EAGER_SIGNALS = True


def build_program():
    nc = bass.Bass("TRN2", target_bir_lowering=False)

    def ck(tag):
        if DBG_STOP == tag:
            P_holder[0].stopped = True

    P_holder = [None]

    def din(name, shape):
        return nc.dram_tensor(name, list(shape), F32, kind="ExternalInput").ap()

    def dout(name, shape):
        return nc.dram_tensor(name, list(shape), F32, kind="ExternalOutput").ap()

    x_d = din("x", [17, 128, 1024])
    sconv_d = din("sconv", [48, 1024])
    sC_d = din("sC", [16, 4, 128, 256])
    snT_d = din("snT", [128, 16, 4])
    m0T_d = din("m0T", [4, 16])
    sS_d = din("sS", [16, 8, 128, 128])
    gcol_d = din("gcol", [128, 8])
    w_in_d = din("w_in", [1024, 7168])
    cw_d = din("cwcol", [128, 8, 4])
    cb_d = din("cbcol", [128, 8])
    wq_d = din("w_q", [4, 256, 128])
    wk_d = din("w_k", [4, 256, 128])
    wv_d = din("w_v", [4, 256, 256])
    wqT_d = din("wqT", [128, 4, 256])
    wkT_d = din("wkT", [128, 4, 256])
    wvT_d = din("wvT", [128, 4, 2, 256])
    wg_d = din("wg", [128, 16, 8])
    bg_d = din("bg", [4, 2])
    mln_d = din("mlncol", [128, 8])
    msk_d = din("mskipcol", [128, 8])
    lbp_d = din("lbp", [128, 2, 8])
    hn_d = din("hnormcol", [128, 8])
    w_out_d = din("w_out", [2048, 1024])
    gfin_d = din("gfin", [128, 1024])

    y_d = dout("y", [17, 128, 1024])
    pconv_d = dout("pconv", [3, 1024])
    pC_d = dout("pC", [4, 128, 256])
    pnT_d = dout("pnT", [128, 4])
    pmT_d = dout("pmT", [4, 1])
    pS_d = dout("pS", [8, 128, 128])
    sconvo_d = dout("sconvo", [16, 3, 1024])
    sCo_d = dout("sCo", [16, 4, 128, 256])
    snTo_d = dout("snTo", [128, 16, 4])
    smTo_d = dout("smTo", [4, 16])
    sSo_d = dout("sSo", [16, 8, 128, 128])

    with ExitStack() as st:
        P = Prog(nc, st)
        P_holder[0] = P

        uniq = [0]
        peak = [0]
        peak_names = []

        live = {}

        def sbt(stack, name, shape, dt):
            uniq[0] += 1
            nbytes = int(np.prod(shape[1:])) * (4 if dt == F32 else 2)
            key = uniq[0]
            live[key] = (name, nbytes)
            stack.callback(lambda: live.pop(key))
            tot = sum(v[1] for v in live.values())
            if tot > peak[0]:
                peak[0] = tot
                peak_names[:] = sorted(live.values(), key=lambda t: -t[1])
            return stack.enter_context(nc.sbuf_tensor("s%d_%s" % (uniq[0], name), list(shape), dt))

        banks = [st.enter_context(nc.psum_tensor(f"bank{i}", [128, 512], F32)) for i in range(8)]
        bank_ctr = [0]

        nrot = [6]

        pinned = set()

        def nb():
            for _t in range(16):
                i = bank_ctr[0] % nrot[0]
                bank_ctr[0] += 1
                if i not in pinned:
                    return i
            raise RuntimeError("no free PSUM bank")
        RES = 7

        def bk(i):
            return "bank%d" % i

        def bbf(i):
            return banks[i][:].bitcast(BF16)

        V = lambda fn, r=(), w=(): P.op("dve", fn, reads=r, writes=w)
        A = lambda fn, r=(), w=(): P.op("act", fn, reads=r, writes=w)
        G = lambda fn, r=(), w=(): P.op("pool", fn, reads=r, writes=w)

        def MM(out, lhsT, rhs, start, stop, r, w, inc):
            P.op("pe", lambda e: e.matmul(out, lhsT=lhsT, rhs=rhs, start=start, stop=stop), reads=r, writes=w, inc=inc)

        def TR(out, in_, ident, r, w, inc):
            P.op("pe", lambda e: e.transpose(out=out, in_=in_, identity=ident), reads=r, writes=w, inc=inc)

        identf = sbt(st, "identf", [128, 128], F32)
        identb = sbt(st, "identb", [128, 128], BF16)
        onesf = sbt(st, "onesf", [128, 128], F32)
        cmask = sbt(st, "cmask", [128, 128], BF16)
        mask2 = sbt(st, "mask2", [128, 128], BF16)
        bmask = sbt(st, "bmask", [128, 128], BF16)
        bm16f = sbt(st, "bm16f", [128, 16], F32)
        bm16 = sbt(st, "bm16", [128, 16], BF16)
        msk64 = sbt(st, "msk64", [128, 1152], BF16)
        gcol = sbt(st, "gcol", [128, 8], F32)
        cwcol = sbt(st, "cwcol", [128, 8, 4], F32)
        cbcol = sbt(st, "cbcol", [128, 8], F32)
        mlncol = sbt(st, "mlncol", [128, 8], F32)
        mskcol = sbt(st, "mskcol", [128, 8], F32)
        hncol = sbt(st, "hncol", [128, 8], F32)
        lbp = sbt(st, "lbp", [128, 2, 8], F32)
        lbc = sbt(st, "lbc", [128, 8], F32)
        omlc = sbt(st, "omlc", [128, 8], F32)
        nomlc = sbt(st, "nomlc", [128, 8], F32)
        homlc = sbt(st, "homlc", [128, 8], F32)
        nhomlc = sbt(st, "nhomlc", [128, 8], F32)
        lbhc = sbt(st, "lbhc", [128, 8], F32)
        mlnh = sbt(st, "mlnh", [128, 8], F32)
        bg = sbt(st, "bg", [4, 2], F32)
        nbgf = sbt(st, "nbgf", [4, 1], F32)
        one4 = sbt(st, "one4", [128, 1], F32)
        epsc = sbt(st, "epsc", [128, 1], F32)
        m0T = sbt(st, "m0T", [4, 16], F32)
        ginit = sbt(st, "ginit", [4, 1], F32)
        minit = sbt(st, "minit", [4, 1], F32)
        GI = sbt(st, "GI", [128, 16, 4], BF16)
        GF = sbt(st, "GF", [128, 16, 4], BF16)
        xnT = sbt(st, "xnT", [128, 8, 1152], BF16)
        mixA = sbt(st, "mixA", [128, 8, 1152], BF16)
        wst = [sbt(st, "wst%d" % i, [128, 8, 512], BF16) for i in range(2)]
        CF = sbt(st, "CF", [128, 4, 257], F32)
        SF = sbt(st, "SF", [128, 8, 128], F32)
        pnT = sbt(st, "pnT", [128, 4], F32)
        snout = sbt(st, "snout", [128, 16, 4], F32)
        nst = sbt(st, "nst", [128, 16, 4], F32)
        tokS = sbt(st, "tokS", [128, 9, 2, 4], F32)
        xmtail = sbt(st, "xmtail", [128, 8, 3], BF16)
        DECb = sbt(st, "DECb", [128, 4, 24], F32)

        wsched = []
        for _p in range(2):
            wsched += [[(0, 512, 0)], [(512, 512, 0)]]
            wsched += [[(1024 + fc * 128, 128, 0), (2048 + fc * 128, 128, 128)] for fc in range(8)]
            wsched += [[(5120, 512, 0)], [(5632, 512, 0)]]
            wsched += [[(3072 + h * 128, 128, 0), (4096 + h * 128, 128, 128), (6144 + h * 128, 128, 256)] for h in range(8)]
        w_issued = [0]
        w_next = [0]

        def issue_w(k):
            sl_ = k % 2
            fns = []
            for (c0, n, d0) in wsched[k]:
                fns.append(lambda e, c0=c0, n=n, d0=d0: e.dma_start(
                    out=wst[sl_][:, :, d0:d0 + n],
                    in_=w_in_d[:, c0:c0 + n].rearrange("(k p) n -> p k n", p=128)))
            P.dma("pool", fns, "wst%d" % sl_, writes=["wst%d" % sl_])

        def acquire_w():
            k = w_next[0]
            w_next[0] += 1
            while w_issued[0] <= min(k + 1, len(wsched) - 1):
                issue_w(w_issued[0])
                w_issued[0] += 1
            return k % 2

        small_loads = [
            (gcol, gcol_d), (cwcol, cw_d), (cbcol, cb_d), (mlncol, mln_d), (mskcol, msk_d),
            (hncol, hn_d), (lbp, lbp_d), (bg, bg_d), (m0T, m0T_d), (nst, snT_d),
        ]
        P.dma("sp", [lambda e, a=a, b=b: e.dma_start(out=a[:], in_=b) for a, b in small_loads], "consts",
              writes=["consts"])

        G(lambda e: e.memset(identf[:], 1.0), w=["identf"])
        G(lambda e: e.affine_select(out=identf[:], in_=identf[:], pattern=[[-1, 128]], compare_op=ALU.is_equal,
                                    fill=0.0, base=0, channel_multiplier=1), r=["identf"], w=["identf"])
        G(lambda e: e.memset(onesf[:], 1.0), w=["onesf"])
        V(lambda e: e.tensor_copy(out=identb[:], in_=identf[:]), r=["identf"], w=["identb"])
        G(lambda e: e.affine_select(out=cmask[:], in_=onesf[:], pattern=[[1, 128]], compare_op=ALU.is_ge,
                                    fill=0.0, base=0, channel_multiplier=-1), r=["onesf"], w=["cmask"])
        V(lambda e: e.tensor_copy(out=mask2[:], in_=cmask[:]), r=["cmask"], w=["mask2"])
        V(lambda e: e.memset(mask2[0:64, 64:128], 0.0), r=["mask2"], w=["mask2"])
        G(lambda e: e.affine_select(out=bm16f[:], in_=onesf[:, 0:16], pattern=[[-8, 16]], compare_op=ALU.is_ge,
                                    fill=0.0, base=0, channel_multiplier=1), r=["onesf"], w=["bm16f"])
        G(lambda e: e.affine_select(out=bm16f[:], in_=bm16f[:], pattern=[[8, 16]], compare_op=ALU.is_ge,
                                    fill=0.0, base=7, channel_multiplier=-1), r=["bm16f"], w=["bm16f"])
        V(lambda e: e.tensor_copy(out=bm16[:], in_=bm16f[:]), r=["bm16f"], w=["bm16"])
        V(lambda e: e.tensor_tensor(out=bmask[:].rearrange("p (b j) -> p b j", j=8),
                                    in0=cmask[:].rearrange("p (b j) -> p b j", j=8),
                                    in1=bm16[:, :].unsqueeze(2).to_broadcast([128, 16, 8]), op=ALU.mult),
          r=["cmask", "bm16"], w=["bmask"])
        G(lambda e: e.memset(msk64[:], 1.0), w=["msk64"])
        G(lambda e: e.memset(msk64[:, 0:1024].rearrange("p (c t) -> p c t", t=64)[:, :, 0:1], 0.0), r=["msk64"], w=["msk64"])
        G(lambda e: e.memset(msk64[:, 1024:1152].rearrange("p (c t) -> p c t", t=8)[:, :, 0:1], 0.0), r=["msk64"], w=["msk64"])
        G(lambda e: e.memset(one4[:], 1.0), w=["one4"])
        G(lambda e: e.memset(epsc[:], EPS), w=["epsc"])
        G(lambda e: e.memset(ginit[:], 0.0), w=["ginit"])
        G(lambda e: e.memset(minit[:], 0.0), w=["minit"])
        G(lambda e: e.memset(CF[:], 0.0), w=["CF0", "CF1", "CF2", "CF3"])
        G(lambda e: e.memset(SF[:], 0.0), w=["SF%d" % h for h in range(8)])
        V(lambda e: e.tensor_tensor(out=lbc[:], in0=lbp[:, 0, :], in1=lbp[:, 1, :], op=ALU.subtract), r=["consts"], w=["lbc"])
        A(lambda e: e.activation(out=lbc[:], in_=lbc[:], func=AF.Sigmoid), r=["lbc"], w=["lbc"])
        V(lambda e: e.tensor_scalar(out=omlc[:], in0=lbc[:], scalar1=-1.0, scalar2=1.0, op0=ALU.mult, op1=ALU.add), r=["lbc"], w=["omlc"])
        V(lambda e: e.tensor_scalar(out=nomlc[:], in0=omlc[:], scalar1=-1.0, scalar2=None, op0=ALU.mult), r=["omlc"], w=["nomlc"])
        V(lambda e: e.tensor_scalar(out=nbgf[:], in0=bg[:, 1:2], scalar1=-1.0, scalar2=None, op0=ALU.mult), r=["consts"], w=["nbgf"])
        V(lambda e: e.tensor_scalar(out=homlc[:], in0=omlc[:], scalar1=0.5, scalar2=None, op0=ALU.mult), r=["omlc"], w=["lbk"])
        V(lambda e: e.tensor_scalar(out=nhomlc[:], in0=omlc[:], scalar1=-0.5, scalar2=None, op0=ALU.mult), r=["omlc", "lbk"], w=["lbk"])
        V(lambda e: e.tensor_tensor(out=lbhc[:], in0=lbc[:], in1=homlc[:], op=ALU.add), r=["lbc", "lbk"], w=["lbk"])
        V(lambda e: e.tensor_scalar(out=mlnh[:], in0=mlncol[:], scalar1=0.5, scalar2=None, op0=ALU.mult), r=["consts"], w=["mlnh"])

        with ExitStack() as s0:
            wqT = sbt(s0, "wqT", [128, 4, 256], F32)
            wkT = sbt(s0, "wkT", [128, 4, 256], F32)
            wvT = sbt(s0, "wvT", [128, 4, 2, 256], F32)
            wg = sbt(s0, "wg", [128, 16, 8], F32)
            P.dma("sp", [lambda e: e.dma_start(out=wqT[:], in_=wqT_d), lambda e: e.dma_start(out=wkT[:], in_=wkT_d),
                         lambda e: e.dma_start(out=wvT[:], in_=wvT_d), lambda e: e.dma_start(out=wg[:], in_=wg_d)],
                  "foldw", writes=["foldw"])
            b = nb()
            for h in range(4):
                for dc in range(2):
                    c = 2 * h + dc
                    MM(banks[b][:, c * 8:(c + 1) * 8], wqT[:, h, dc * 128:(dc + 1) * 128], wg[:, h, :], True, False,
                       ["foldw"], [bk(b)], False)
                    MM(banks[b][:, c * 8:(c + 1) * 8], wkT[:, h, dc * 128:(dc + 1) * 128], wg[:, 4 + h, :], False, True,
                       ["foldw"], [bk(b)], False)
                    c2 = 8 + c
                    for vc in range(2):
                        MM(banks[b][:, c2 * 8:(c2 + 1) * 8], wvT[:, h, vc, dc * 128:(dc + 1) * 128], wg[:, 8 + 2 * h + vc, :],
                           vc == 0, vc == 1, ["foldw"], [bk(b)], (c == 7 and vc == 1))
            pv = banks[b][:, 0:128].rearrange("p (c g) -> p c g", g=8)
            V(lambda e: e.tensor_copy(out=GI[:], in_=pv[:, :, 0:4]), r=[bk(b)], w=["GI"])
            V(lambda e: e.tensor_copy(out=GF[:], in_=pv[:, :, 4:8]), r=[bk(b)], w=["GF"])
            P.full_barrier()

        issue_w(0)
        w_issued[0] = 1
        ck("consts")
        for ps_ in range(2):
            NTp = 8 if ps_ == 0 else 9
            W = NTp * 128
            has_s = (ps_ == 1)
            tgs = [(0, 512), (512, 512)] + ([(1024, 128)] if has_s else [])
            gt = lambda i: (ps_ * 8 + i) if i < 8 else 16

            with ExitStack() as s1:
                xt = [sbt(s1, "xt%d" % i, [128, 1024], F32) for i in range(3)]
                xsb = [sbt(s1, "xsb%d" % i, [128, 1024], BF16) for i in range(2)]
                junk = sbt(s1, "junk", [128, 1024], BF16)
                ss1 = sbt(s1, "ss1", [128, 9], F32)
                rs1 = sbt(s1, "rs1", [128, 9], F32)
                for i in range(NTp):
                    sl = i % 3
                    s2 = i % 2
                    P.dma("sp", lambda e, i=i, sl=sl: e.dma_start(out=xt[sl][:], in_=x_d[gt(i)]), "xt%d" % sl,
                          writes=["xt%d" % sl])
                    A(lambda e, i=i, sl=sl: e.activation(out=junk[:], in_=xt[sl][:], func=AF.Square,
                                                          accum_out=ss1[:, i:i + 1]),
                      r=["xt%d" % sl], w=["junk", "ss1_%d" % i])
                    A(lambda e, i=i: e.activation(out=rs1[:, i:i + 1], in_=ss1[:, i:i + 1], func=AF.Ln,
                                                  scale=1.0 / 1024.0, bias=epsc[:, 0:1]),
                      r=["ss1_%d" % i, "epsc"], w=["rs1_%d" % i])
                    A(lambda e, i=i: e.activation(out=rs1[:, i:i + 1], in_=rs1[:, i:i + 1], func=AF.Exp, scale=-0.5),
                      r=["rs1_%d" % i], w=["rs1_%d" % i])
                    V(lambda e, i=i, sl=sl, s2=s2: e.tensor_scalar(out=xsb[s2][:], in0=xt[sl][:], scalar1=rs1[:, i:i + 1],
                                                                   scalar2=None, op0=ALU.mult),
                      r=["xt%d" % sl, "rs1_%d" % i], w=["xsb%d" % s2])
                    b = nb()
                    for k in range(8):
                        TR(bbf(b)[:, k * 128:(k + 1) * 128], xsb[s2][:, k * 128:(k + 1) * 128], identb[:],
                           ["xsb%d" % s2, "identb"], [bk(b)], k == 7)
                    V(lambda e, i=i, b=b: e.tensor_tensor(out=xnT[:, :, i * 128:(i + 1) * 128],
                                                          in0=bbf(b).rearrange("p (k t) -> p k t", k=8),
                                                          in1=gcol[:, :].unsqueeze(2).to_broadcast([128, 8, 128]), op=ALU.mult),
                      r=[bk(b), "consts"], w=["xnT_%d" % i])
                P.full_barrier()
            ck("p%d_ph1" % ps_)
            xn_all = ["xnT_%d" % i for i in range(NTp)]

            def xn_keys(c0, n):
                return ["xnT_%d" % i for i in range(c0 // 128, (c0 + n) // 128)]

            def inproj_fm(slot, dcol, evac):
                for (c0, n) in tgs:
                    b = nb()
                    for k in range(8):
                        MM(banks[b][:, 0:n], wst[slot][:, k, dcol:dcol + 128], xnT[:, k, c0:c0 + n], k == 0, k == 7,
                           ["wst%d" % slot] + xn_keys(c0, n), [bk(b)], k == 7)
                    evac(b, c0, n)

            with ExitStack() as s2_:
                wq_bf = sbt(s2_, "wq_bf", [128, 4, 2, 128], BF16)
                wk_bf = sbt(s2_, "wk_bf", [128, 4, 2, 128], BF16)
                wv_bf = sbt(s2_, "wv_bf", [128, 4, 2, 256], BF16)
                P.dma("pool", [
                    lambda e: e.dma_start(out=wq_bf[:], in_=wq_d.rearrange("h (c p) k -> p h c k", p=128)),
                    lambda e: e.dma_start(out=wk_bf[:], in_=wk_d.rearrange("h (c p) k -> p h c k", p=128)),
                    lambda e: e.dma_start(out=wv_bf[:], in_=wv_d.rearrange("h (c p) k -> p h c k", p=128)),
                ], "wsmall", writes=["wqkv"])
                xmT = sbt(s2_, "xmT", [128, 8, 1028], BF16)
                xmS = sbt(s2_, "xmS", [128, 8, 11, 16], BF16)
                xmSc = sbt(s2_, "xmSc", [128, 8, 128], BF16)
                xcT = sbt(s2_, "xcT", [128, 8, 1152], BF16)
                dg = [sbt(s2_, "dg%d" % i, [128, 4, 128], BF16) for i in range(2)]
                xmtok = [sbt(s2_, "xmtok%d" % i, [128, 1024], F32) for i in range(2)] if has_s else None

                if ps_ == 0:
                    V(lambda e: e.memset(xmT[:, :, 0:4], 0.0), w=["xmTpad"])
                else:
                    V(lambda e: e.tensor_copy(out=xmT[:, :, 1:4], in_=xmtail[:]), r=["xmtail"], w=["xmTpad"])
                    with ExitStack() as s2a:
                        sct = sbt(s2a, "sct", [48, 1024], F32)
                        P.dma("sp", lambda e: e.dma_start(out=sct[:], in_=sconv_d), "sct", writes=["sct"])
                        b = nb()
                        for k in range(8):
                            TR(banks[b][:, k * 48:(k + 1) * 48], sct[0:48, k * 128:(k + 1) * 128], identf[0:48, 0:48],
                               ["sct", "identf"], [bk(b)], k == 7)
                        V(lambda e, b=b: e.tensor_copy(out=xmS[:, :, 0:3, :],
                                                       in_=banks[b][:, 0:384].rearrange("p (k b j) -> p k j b", k=8, b=16)),
                          r=[bk(b)], w=["xmSpad"])
                        P.full_barrier()

                def xm_keys(k):
                    return ["xmT_%d" % k, "xmTpad", "xmSpad"]

                def conv_chunk(kc):
                    d = kc % 2
                    V(lambda e, d=d, kc=kc: e.tensor_tensor(
                        out=dg[d][:], in0=identb[:, :].unsqueeze(1).to_broadcast([128, 4, 128]),
                        in1=cwcol[:, kc, :].unsqueeze(2).to_broadcast([128, 4, 128]), op=ALU.mult),
                      r=["identb", "consts"], w=["dg%d" % d])
                    for (c0, n) in tgs:
                        b = nb()
                        for j in range(4):
                            rhs = (xmT[:, kc, c0 + j + 1:c0 + j + 1 + n] if c0 < 1024 else
                                   xmS[:, kc, :, :].rearrange("p t b -> p (t b)")[:, j * 16:j * 16 + 128])
                            MM(banks[b][:, 0:n], dg[d][:, j, :], rhs, j == 0, j == 3,
                               ["dg%d" % d] + xm_keys(kc), [bk(b)], j == 3)
                        if c0 < 1024:
                            A(lambda e, b=b, c0=c0, n=n, kc=kc: e.activation(out=xcT[:, kc, c0:c0 + n], in_=banks[b][:, 0:n],
                                                                              func=AF.Silu, bias=cbcol[:, kc:kc + 1]),
                              r=[bk(b), "consts"], w=["xcT_%d" % kc])
                        else:
                            A(lambda e, b=b, kc=kc: e.activation(out=xcT[:, kc, 1024:1152].rearrange("p (b t) -> p b t", t=8),
                                                                  in_=banks[b][:, 0:128].rearrange("p (t b) -> p b t", b=16),
                                                                  func=AF.Silu, bias=cbcol[:, kc:kc + 1]),
                              r=[bk(b), "consts"], w=["xcT_%d" % kc])

                for blk in range(2):
                    slot = acquire_w()
                    for sub in range(4):
                        kc = blk * 4 + sub

                        def ev_xm(b, c0, n, kc=kc):
                            if c0 < 1024:
                                A(lambda e: e.activation(func=AF.Copy, out=xmT[:, kc, 4 + c0:4 + c0 + n], in_=banks[b][:, 0:n]),
                                  r=[bk(b)], w=["xmT_%d" % kc])
                            else:
                                A(lambda e: e.activation(func=AF.Copy, out=xmS[:, kc, 3:11, :],
                                                   in_=banks[b][:, 0:128].rearrange("p (b j) -> p j b", j=8)),
                                  r=[bk(b)], w=["xmT_%d" % kc])
                                A(lambda e: e.activation(func=AF.Copy, out=xmSc[:, kc, :], in_=banks[b][:, 0:128]), r=[bk(b)], w=["xmSc_%d" % kc])
                        inproj_fm(slot, sub * 128, ev_xm)
                        if kc > 0:
                            conv_chunk(kc - 1)
                    if has_s:
                        for ti, c0 in ((0, 896), (1, 1024)):
                            b = nb()
                            for k in range(8):
                                MM(banks[b][:, :], xnT[:, k, c0:c0 + 128], wst[slot][:, k, :], k == 0, k == 7,
                                   ["wst%d" % slot] + xn_keys(c0, 128), [bk(b)], k == 7)
                            A(lambda e, b=b, ti=ti, blk=blk: e.activation(func=AF.Copy, out=xmtok[ti][:, blk * 512:(blk + 1) * 512], in_=banks[b][:, :]),
                              r=[bk(b)], w=["xmtok%d_%d" % (ti, blk)])
                conv_chunk(7)
                if has_s:
                    P.dma("sp", lambda e: e.dma_start(out=pconv_d, in_=xmtok[0][125:128, :]), "pconv",
                          reads=["xmtok0_0", "xmtok0_1"])
                    P.dma("sp", [lambda e, b_=b_: e.dma_start(out=sconvo_d[b_], in_=xmtok[1][8 * b_ + 5:8 * b_ + 8, :])
                                 for b_ in range(16)], "sconvo", reads=["xmtok1_0", "xmtok1_1"])

                ck("p%d_ph2a" % ps_)
                xc_all = ["xcT_%d" % k for k in range(8)]
                xm_all = ["xmT_%d" % k for k in range(8)] + ["xmTpad", "xmSpad"]

                with ExitStack() as sg:
                    mskg = sbt(sg, "mskg", [4, 1152], F32)
                    bigm = sbt(sg, "bigm", [4, 128], F32)
                    G(lambda e: e.memset(mskg[:], 1.0), w=["mskg"])
                    G(lambda e: e.memset(mskg[:, 1024:1152].rearrange("p (c t) -> p c t", t=8)[:, :, 0:1], 0.0), r=["mskg"], w=["mskg"])
                    G(lambda e: e.memset(bigm[:], 1e30), w=["bigm"])
                    G(lambda e: e.memset(bigm[:, :].rearrange("p (c t) -> p c t", t=8)[:, :, 0:1], -1e30), r=["bigm"], w=["bigm"])
                    IG = sbt(sg, "IG", [4, 1152], F32)
                    SPt = sbt(sg, "SPt", [4, 1152], F32)
                    GN = sbt(sg, "GN", [4, 1152], F32)
                    AG = sbt(sg, "AG", [4, 1152], F32)
                    MT = sbt(sg, "MT", [4, 1152], F32)
                    T1 = sbt(sg, "T1", [4, 1152], F32)
                    T3 = sbt(sg, "T3", [4, 1152], F32)
                    Rall = sbt(sg, "Rall", [4, 24], F32)
                    Rprev = sbt(sg, "Rprev", [4, 24], F32)
                    DARG = sbt(sg, "DARG", [4, 24], F32)
                    Dblk = sbt(sg, "Dblk", [4, 4, 24], F32)
                    mo = sbt(sg, "mo", [4, 17], F32)
                    for (c0, n) in tgs:
                        bi, bf_ = nb(), nb()
                        for (bb, Gw) in ((bi, GI), (bf_, GF)):
                            for c in range(16):
                                kc = c % 8
                                if c < 8:
                                    rhs = xcT[:, kc, c0:c0 + n]
                                    rk = ["xcT_%d" % kc]
                                else:
                                    rhs = xmT[:, kc, 4 + c0:4 + c0 + n] if c0 < 1024 else xmSc[:, kc, :]
                                    rk = ["xmT_%d" % kc, "xmSc_%d" % kc]
                                MM(banks[bb][0:4, 0:n], Gw[:, c, :], rhs, c == 0, c == 15, rk + ["GI", "GF"], [bk(bb)], c == 15)
                        A(lambda e, bi=bi, c0=c0, n=n: e.activation(out=IG[:, c0:c0 + n], in_=banks[bi][0:4, 0:n], func=AF.Identity,
                                                                     bias=bg[:, 0:1]), r=[bk(bi), "consts"], w=["IG"])
                        A(lambda e, bf_=bf_, c0=c0, n=n: e.activation(out=SPt[:, c0:c0 + n], in_=banks[bf_][0:4, 0:n], func=AF.Exp,
                                                                       scale=-1.0, bias=nbgf[:, 0:1]), r=[bk(bf_), "nbgf"], w=["SPt"])
                    A(lambda e: e.activation(out=SPt[:, 0:W], in_=SPt[:, 0:W], func=AF.Ln, bias=one4[0:4, 0:1]), r=["SPt", "one4"], w=["SPt"])
                    V(lambda e: e.tensor_tensor_scan(out=GN[:, 0:W], data0=mskg[:, 0:W], data1=SPt[:, 0:W], initial=ginit[:, 0:1],
                                                     op0=ALU.mult, op1=ALU.add), r=["mskg", "SPt", "ginit"], w=["GN"])
                    V(lambda e: e.tensor_tensor(out=AG[:, 0:W], in0=IG[:, 0:W], in1=GN[:, 0:W], op=ALU.add), r=["IG", "GN"], w=["AG"])
                    V(lambda e: e.tensor_tensor_scan(out=MT[:, 0:1024], data0=AG[:, 0:1024], data1=AG[:, 0:1024], initial=minit[:, 0:1],
                                                     op0=ALU.max, op1=ALU.max), r=["AG", "minit"], w=["MT"])
                    if has_s:
                        agv = AG[:, 1024:1152].rearrange("p (b j) -> p b j", j=8)
                        V(lambda e: e.tensor_tensor(out=agv[:, :, 0:1], in0=agv[:, :, 0:1], in1=m0T[:, :].unsqueeze(2), op=ALU.max),
                          r=["AG", "consts"], w=["AGs"])
                        V(lambda e: e.tensor_tensor_scan(out=MT[:, 1024:1152], data0=bigm[:, :], data1=AG[:, 1024:1152], initial=0.0,
                                                         op0=ALU.min, op1=ALU.max), r=["AGs", "AG", "bigm"], w=["MTs"])
                        V(lambda e: e.tensor_tensor(out=AG[:, 1024:1152], in0=IG[:, 1024:1152], in1=GN[:, 1024:1152], op=ALU.add),
                          r=["IG", "GN", "MTs"], w=["AG"])
                    mtk = ["MT", "MTs"]
                    V(lambda e: e.tensor_copy(out=Rall[:, 0:8].unsqueeze(2), in_=MT[:, 0:1024].rearrange("p (c t) -> p c t", t=128)[:, :, 127:128]),
                      r=mtk, w=["Rall"])
                    V(lambda e: e.tensor_copy(out=Rprev[:, 0:1], in_=minit[:, 0:1]), r=["minit"], w=["Rprev"])
                    V(lambda e: e.tensor_copy(out=Rprev[:, 1:8], in_=Rall[:, 0:7]), r=["Rall", "Rprev"], w=["Rprev"])
                    if has_s:
                        V(lambda e: e.tensor_copy(out=Rall[:, 8:24].unsqueeze(2), in_=MT[:, 1024:1152].rearrange("p (b j) -> p b j", j=8)[:, :, 7:8]),
                          r=mtk + ["Rall"], w=["Rall"])
                        V(lambda e: e.tensor_copy(out=Rprev[:, 8:24], in_=m0T[:, :]), r=["consts", "Rprev"], w=["Rprev"])
                    else:
                        V(lambda e: e.memset(Rall[:, 8:24], 0.0), r=["Rall"], w=["Rall"])
                        V(lambda e: e.memset(Rprev[:, 8:24], 0.0), r=["Rprev"], w=["Rprev"])
                    for (src, dst, nm) in ((AG, T1, "T1"), (GN, T3, "T3")):
                        V(lambda e, src=src, dst=dst: e.tensor_tensor(
                            out=dst[:, 0:1024].rearrange("p (c t) -> p c t", t=128), in0=src[:, 0:1024].rearrange("p (c t) -> p c t", t=128),
                            in1=Rall[:, 0:8].unsqueeze(2).to_broadcast([4, 8, 128]), op=ALU.subtract), r=["AG", "GN", "Rall"], w=[nm])
                        if has_s:
                            V(lambda e, src=src, dst=dst: e.tensor_tensor(
                                out=dst[:, 1024:1152].rearrange("p (c t) -> p c t", t=8), in0=src[:, 1024:1152].rearrange("p (c t) -> p c t", t=8),
                                in1=Rall[:, 8:24].unsqueeze(2).to_broadcast([4, 16, 8]), op=ALU.subtract), r=["AG", "GN", "Rall", nm], w=[nm])
                        A(lambda e, dst=dst: e.activation(out=dst[:, 0:W], in_=dst[:, 0:W], func=AF.Exp), r=[nm], w=[nm])
                    V(lambda e: e.tensor_tensor(out=DARG[:], in0=Rprev[:], in1=Rall[:], op=ALU.subtract), r=["Rprev", "Rall"], w=["DARG"])
                    if has_s:
                        V(lambda e: e.tensor_tensor(out=mo[:, 0:1], in0=MT[:, 1023:1024], in1=GN[:, 1023:1024], op=ALU.subtract),
                          r=mtk + ["GN"], w=["mo"])
                        V(lambda e: e.tensor_tensor(out=mo[:, 1:17].unsqueeze(2), in0=MT[:, 1024:1152].rearrange("p (b j) -> p b j", j=8)[:, :, 7:8],
                                                    in1=GN[:, 1024:1152].rearrange("p (b j) -> p b j", j=8)[:, :, 7:8], op=ALU.subtract),
                          r=mtk + ["GN", "mo"], w=["mo"])
                        P.dma("sp", [lambda e: e.dma_start(out=pmT_d, in_=mo[:, 0:1]),
                                     lambda e: e.dma_start(out=smTo_d, in_=mo[:, 1:17])], "mo", reads=["mo"])
                    V(lambda e: e.tensor_copy(out=ginit[:, 0:1], in_=GN[:, 1023:1024]), r=["GN", "Rprev"], w=["ginit"])
                    V(lambda e: e.tensor_copy(out=minit[:, 0:1], in_=MT[:, 1023:1024]), r=mtk + ["Rprev"], w=["minit"])
                    b = nb()
                    for i in range(NTp):
                        for qi, Q in enumerate((T1, T3)):
                            o = (i * 2 + qi) * 4
                            TR(banks[b][:, o:o + 4], Q[0:4, i * 128:(i + 1) * 128], identf[0:4, 0:4], ["T1", "T3", "identf"], [bk(b)],
                               (i == NTp - 1 and qi == 1))
                    V(lambda e, b=b: e.tensor_copy(out=tokS[:, 0:NTp].rearrange("p i q h -> p (i q h)"), in_=banks[b][:, 0:NTp * 8]),
                      r=[bk(b)], w=["tokS"])
                    V(lambda e: e.tensor_tensor(out=Dblk[:], in0=DARG[:, :].unsqueeze(1).to_broadcast([4, 4, 24]),
                                                in1=identf[0:4, 0:4].unsqueeze(2).to_broadcast([4, 4, 24]), op=ALU.mult),
                      r=["DARG", "identf"], w=["Dblk"])
                    b = nb()
                    MM(banks[b][:, 0:96], onesf[0:4, :], Dblk[:].rearrange("p h c -> p (h c)"), True, True, ["onesf", "Dblk"], [bk(b)], True)
                    A(lambda e, b=b: e.activation(out=DECb[:].rearrange("p h c -> p (h c)"), in_=banks[b][:, 0:96], func=AF.Exp),
                      r=[bk(b)], w=["DECb"])
                    P.full_barrier()

                ck("p%d_gates" % ps_)
                with ExitStack() as sh:
                    HS = []
                    NHS = 4
                    for p_ in range(NHS):
                        HS.append(dict(
                            qT=sbt(sh, "qT%d" % p_, [128, 1152], BF16), kT=sbt(sh, "kT%d" % p_, [128, 1152], BF16),
                            ktil=sbt(sh, "ktil%d" % p_, [128, 9, 128], BF16), v1=sbt(sh, "v1%d" % p_, [128, 9, 258], BF16),
                            sTsb=[sbt(sh, "sTsb%d_%d" % (p_, i), [128, 128], BF16) for i in range(2)],
                            Cd=sbt(sh, "Cd%d" % p_, [128, 258], BF16),
                            hnsb=[sbt(sh, "hnsb%d_%d" % (p_, i), [128, 256], BF16) for i in range(2)],
                            st6=sbt(sh, "st6%d" % p_, [128, 2, 6], F32), mv=sbt(sh, "mv%d" % p_, [128, 2, 2], F32),
                            wv=sbt(sh, "wv%d" % p_, [128, 2, 4], F32)))
                        G(lambda e, p_=p_: e.memset(HS[p_]["v1"][:, :, 256:257], 1.0), w=["v1ones%d" % p_])
                    if has_s:
                        Csts = [sbt(sh, "Cst%d" % i, [128, 4, 257], F32) for i in range(4)]

                        def issue_cst(r_):
                            h_, g_ = r_ // 4, r_ % 4
                            cb = Csts[r_ % 4]
                            P.dma("sp", lambda e: e.dma_start(
                                out=cb[:, :, 0:256], in_=sC_d[g_ * 4:(g_ + 1) * 4, h_].rearrange("b k v -> k b v")),
                                "Cst%d" % (r_ % 4), writes=["Cst%d" % (r_ % 4)])
                        Cdb = sbt(sh, "Cdb", [128, 4, 258], BF16)
                        Qpad = [sbt(sh, "Qpad%d" % i, [128, 640], BF16) for i in range(4)]
                        Kpad = sbt(sh, "Kpad", [128, 4, 128], BF16)
                        for i_q in range(4):
                            G(lambda e: e.memset(Qpad[i_q][:], 0.0), w=["Qpad%d" % i_q])

                    def out_stage(h, i, bnum, par):
                        B = HS[h % NHS]
                        st6, mv, wv_, hnsb = B["st6"], B["mv"], B["wv"], B["hnsb"]
                        pn_ = banks[bnum]
                        kq = "ost%d_%d" % (h % NHS, par)
                        hk = "hnsb%d_%d" % (h % NHS, par)
                        V(lambda e: e.bn_stats(out=st6[:, par, :], in_=pn_[:, 0:256]), r=[bk(bnum)], w=[kq + "a"])
                        V(lambda e: e.bn_aggr(out=mv[:, par, :], in_=st6[:, par, :]), r=[kq + "a"], w=[kq + "b"])
                        V(lambda e: e.tensor_scalar(out=wv_[:, par, 1:2], in0=pn_[:, 256:257], scalar1=-1.0, scalar2=None, op0=ALU.mult),
                          r=[bk(bnum)], w=[kq + "c0"])
                        V(lambda e: e.scalar_tensor_tensor(out=wv_[:, par, 0:1], in0=pn_[:, 256:257], scalar=tokS[:, i, 1, h:h + 1],
                                                           in1=wv_[:, par, 1:2], op0=ALU.max, op1=ALU.max),
                          r=[bk(bnum), "tokS", kq + "c0"], w=[kq + "c"])
                        V(lambda e: e.tensor_tensor(out=wv_[:, par, 1:2], in0=wv_[:, par, 0:1], in1=wv_[:, par, 0:1], op=ALU.mult),
                          r=[kq + "c", kq + "c0"], w=[kq + "d", kq + "c0"])
                        V(lambda e: e.scalar_tensor_tensor(out=wv_[:, par, 2:3], in0=wv_[:, par, 1:2], scalar=EPS, in1=mv[:, par, 1:2],
                                                           op0=ALU.mult, op1=ALU.add), r=[kq + "d", kq + "b"], w=[kq + "e"])
                        A(lambda e: e.activation(out=wv_[:, par, 3:4], in_=wv_[:, par, 2:3], func=AF.Ln), r=[kq + "e"], w=[kq + "f"])
                        A(lambda e: e.activation(out=wv_[:, par, 3:4], in_=wv_[:, par, 3:4], func=AF.Exp, scale=-0.5), r=[kq + "f"], w=[kq + "f"])
                        V(lambda e: e.tensor_scalar(out=hnsb[par][:], in0=pn_[:, 0:256], scalar1=mv[:, par, 0:1], scalar2=wv_[:, par, 3:4],
                                                    op0=ALU.subtract, op1=ALU.mult), r=[bk(bnum), kq + "b", kq + "f"], w=[hk])
                        bt = nb()
                        for c in range(2):
                            TR(bbf(bt)[:, c * 128:(c + 1) * 128], hnsb[par][:, c * 128:(c + 1) * 128], identb[:], [hk, "identb"],
                               [bk(bt)], c == 1)
                        A(lambda e: e.activation(func=AF.Copy, out=mixA[:, 2 * h:2 * h + 2, i * 128:(i + 1) * 128],
                                                 in_=bbf(bt)[:, 0:256].rearrange("p (c t) -> p c t", c=2)), r=[bk(bt)], w=["mixA_%d" % h])

                    D4 = sbt(sh, "D4", [128, 2, 4], F32)
                    mv4 = sbt(sh, "mv4", [128, 2, 4, 2], F32)
                    w4 = sbt(sh, "w4", [128, 2, 4, 4], F32)
                    pend = {}

                    def out_stage4(i):
                        par = i % 2
                        kq = "os4_%d" % par
                        for h in range(4):
                            B = HS[h]
                            pn_ = banks[pend[h]]
                            V(lambda e: e.bn_stats(out=B["st6"][:, par, :], in_=pn_[:, 0:256]), r=[bk(pend[h])], w=[kq + "s%d" % h])
                            V(lambda e: e.bn_aggr(out=mv4[:, par, h, :], in_=B["st6"][:, par, :]), r=[kq + "s%d" % h], w=[kq + "mv%d" % h])
                            V(lambda e: e.tensor_copy(out=D4[:, par, h:h + 1], in_=pn_[:, 256:257]), r=[bk(pend[h])],
                              w=[kq + "d%d" % h])
                        dk = [kq + "d%d" % h for h in range(4)]
                        mk = [kq + "mv%d" % h for h in range(4)]
                        wa, wb, wc, wd = (w4[:, par, :, j] for j in range(4))
                        V(lambda e: e.tensor_scalar(out=wa, in0=D4[:, par, :], scalar1=-1.0, scalar2=None, op0=ALU.mult), r=dk, w=[kq + "a"])
                        V(lambda e: e.tensor_tensor(out=wb, in0=D4[:, par, :], in1=tokS[:, i, 1, :], op=ALU.max), r=dk + ["tokS"], w=[kq + "b"])
                        V(lambda e: e.tensor_tensor(out=wb, in0=wb, in1=wa, op=ALU.max), r=[kq + "a", kq + "b"], w=[kq + "b"])
                        V(lambda e: e.tensor_tensor(out=wc, in0=wb, in1=wb, op=ALU.mult), r=[kq + "b"], w=[kq + "c"])
                        V(lambda e: e.scalar_tensor_tensor(out=wc, in0=wc, scalar=EPS, in1=mv4[:, par, :, 1], op0=ALU.mult, op1=ALU.add),
                          r=[kq + "c"] + mk, w=[kq + "c"])
                        A(lambda e: e.activation(out=wd, in_=wc, func=AF.Ln), r=[kq + "c"], w=[kq + "e"])
                        A(lambda e: e.activation(out=wd, in_=wd, func=AF.Exp, scale=-0.5), r=[kq + "e"], w=[kq + "e"])
                        for h in range(4):
                            B = HS[h]
                            pn_ = banks[pend[h]]
                            hk = "hnsb%d_%d" % (h, par)
                            hnsb = B["hnsb"]
                            V(lambda e: e.tensor_scalar(out=hnsb[par][:], in0=pn_[:, 0:256], scalar1=mv4[:, par, h, 0:1],
                                                        scalar2=w4[:, par, h, 3:4], op0=ALU.subtract, op1=ALU.mult),
                              r=[bk(pend[h]), kq + "e"] + mk, w=[hk])
                            bt = nb()
                            for c in range(2):
                                TR(bbf(bt)[:, c * 128:(c + 1) * 128], hnsb[par][:, c * 128:(c + 1) * 128], identb[:], [hk, "identb"],
                                   [bk(bt)], c == 1)
                            A(lambda e: e.activation(func=AF.Copy, out=mixA[:, 2 * h:2 * h + 2, i * 128:(i + 1) * 128],
                                                     in_=bbf(bt)[:, 0:256].rearrange("p (c t) -> p c t", c=2)), r=[bk(bt)], w=["mixA_%d" % h])

                    def proj(h):
                        B = HS[h % NHS]
                        p_ = h % NHS
                        qT, kT, ktil, v1 = B["qT"], B["kT"], B["ktil"], B["v1"]
                        for (c0, n) in tgs:
                            bq, bk_ = nb(), nb()
                            for dc in range(2):
                                MM(banks[bq][:, 0:n], wq_bf[:, h, dc, :], xcT[:, 2 * h + dc, c0:c0 + n], dc == 0, dc == 1,
                                   ["wqkv", "xcT_%d" % (2 * h + dc)], [bk(bq)], dc == 1)
                            for dc in range(2):
                                MM(banks[bk_][:, 0:n], wk_bf[:, h, dc, :], xcT[:, 2 * h + dc, c0:c0 + n], dc == 0, dc == 1,
                                   ["wqkv", "xcT_%d" % (2 * h + dc)], [bk(bk_)], dc == 1)
                            A(lambda e: e.activation(out=qT[:, c0:c0 + n], in_=banks[bq][:, 0:n], func=AF.Copy,
                                                     scale=float(128 ** -0.5)), r=[bk(bq)], w=["qT%d" % p_])
                            V(lambda e: e.tensor_copy(out=kT[:, c0:c0 + n], in_=banks[bk_][:, 0:n]), r=[bk(bk_)], w=["kT%d" % p_])
                        for i in range(NTp):
                            b = nb()
                            cs = slice(i * 128, (i + 1) * 128)
                            for dc in range(2):
                                lhs = xmT[:, 2 * h + dc, 4 + i * 128:4 + (i + 1) * 128] if i < 8 else xmSc[:, 2 * h + dc, :]
                                MM(banks[b][:, 0:256], lhs, wv_bf[:, h, dc, :], dc == 0, dc == 1,
                                   ["wqkv", "xmT_%d" % (2 * h + dc), "xmSc_%d" % (2 * h + dc)], [bk(b)], False)
                            for dc in range(2):
                                MM(banks[b][:, 256:384], xcT[:, 2 * h + dc, cs], wk_bf[:, h, dc, :], dc == 0, dc == 1,
                                   ["wqkv", "xcT_%d" % (2 * h + dc)], [bk(b)], dc == 1)
                            V(lambda e: e.tensor_scalar(out=ktil[:, i, :], in0=banks[b][:, 256:384], scalar1=tokS[:, i, 0, h:h + 1],
                                                        scalar2=None, op0=ALU.mult), r=[bk(b), "tokS"], w=["ktil%d_%d" % (p_, i)])
                            V(lambda e: e.tensor_copy(out=v1[:, i, 0:256], in_=banks[b][:, 0:256]), r=[bk(b)],
                              w=["v1%d_%d" % (p_, i)])

                    def mloop(h):
                        B = HS[h % NHS]
                        p_ = h % NHS
                        qT, kT, ktil, v1, sTsb, Cd = B["qT"], B["kT"], B["ktil"], B["v1"], B["sTsb"], B["Cd"]
                        cfk = "CF%d" % h
                        for i in range(8):
                            gi = ps_ * 8 + i
                            cs = slice(i * 128, (i + 1) * 128)
                            par = i % 2
                            sk = "sTsb%d_%d" % (p_, par)
                            vk = ["v1%d_%d" % (p_, i), "v1ones%d" % p_]
                            ba = nb()
                            MM(banks[ba][:, 0:128], kT[:, cs], qT[:, cs], True, True, ["kT%d" % p_, "qT%d" % p_], [bk(ba)], True)
                            V(lambda e: e.scalar_tensor_tensor(out=sTsb[par][:], in0=banks[ba][:, 0:128],
                                                               scalar=tokS[:, i, 0, h:h + 1], in1=cmask[:],
                                                               op0=ALU.mult, op1=ALU.mult),
                              r=[bk(ba), "tokS", "cmask"], w=[sk])
                            if gi > 0:
                                A(lambda e: e.activation(out=Cd[:, 0:257], in_=CF[:, h, :], func=AF.Copy, scale=DECb[:, h, i:i + 1]),
                                  r=[cfk, "DECb"], w=["Cd%d" % p_])
                            bn_ = 3 + h
                            MM(banks[bn_][:, 0:257], sTsb[par][:], v1[:, i, 0:257], True, gi == 0, [sk] + vk, [bk(bn_)], gi == 0)
                            if gi > 0:
                                MM(banks[bn_][:, 0:257], qT[:, cs], Cd[:, 0:257], False, True, ["qT%d" % p_, "Cd%d" % p_], [bk(bn_)], True)
                            bc = nb()
                            MM(banks[bc][:, 0:257], ktil[:, i, :], v1[:, i, 0:257], True, True, ["ktil%d_%d" % (p_, i)] + vk, [bk(bc)], True)
                            V(lambda e: e.scalar_tensor_tensor(out=CF[:, h, :], in0=CF[:, h, :], scalar=DECb[:, h, i:i + 1],
                                                               in1=banks[bc][:, 0:257], op0=ALU.mult, op1=ALU.add),
                              r=[cfk, "DECb", bk(bc)], w=[cfk])
                            pend[h] = bn_
                            yield

                    def msample(h):
                        B = HS[h % NHS]
                        p_ = h % NHS
                        qT, kT, ktil, v1, sTsb = B["qT"], B["kT"], B["ktil"], B["v1"], B["sTsb"]
                        cfk = "CF%d" % h
                        P.dma("sp", lambda e: e.dma_start(out=pC_d[h], in_=CF[:, h, 0:256]), "pC", reads=[cfk])
                        V(lambda e: e.tensor_copy(out=pnT[:, h:h + 1], in_=CF[:, h, 256:257]), r=[cfk], w=["pnT"])
                        cs = slice(1024, 1152)
                        sk = "sTsb%d_0" % p_
                        vk = ["v1%d_8" % p_, "v1ones%d" % p_]
                        ba = nb()
                        MM(banks[ba][:, 0:128], kT[:, cs], qT[:, cs], True, True, ["kT%d" % p_, "qT%d" % p_], [bk(ba)], True)
                        V(lambda e: e.scalar_tensor_tensor(out=sTsb[0][:], in0=banks[ba][:, 0:128], scalar=tokS[:, 8, 0, h:h + 1],
                                                           in1=bmask[:], op0=ALU.mult, op1=ALU.mult),
                          r=[bk(ba), "tokS", "bmask"], w=[sk])
                        bn_ = RES
                        MM(banks[bn_][:, 0:257], sTsb[0][:], v1[:, 8, 0:257], True, False, [sk] + vk, [bk(bn_)], False)
                        for grp in range(4):
                            r_ = 4 * h + grp
                            if r_ == 0:
                                issue_cst(0)
                                issue_cst(1)
                            if r_ + 2 < 16:
                                issue_cst(r_ + 2)
                            Cst = Csts[r_ % 4]
                            ck_, cnk_ = "Cst%d" % (r_ % 4), "Cstn%d" % (r_ % 4)
                            V(lambda e: e.tensor_copy(out=Cst[:, :, 256:257], in_=nst[:, grp * 4:(grp + 1) * 4, h:h + 1]),
                              r=["consts", ck_], w=[cnk_])
                            V(lambda e: e.tensor_tensor(
                                out=Cdb[:, :, 0:257], in0=Cst[:], in1=DECb[:, h, 8 + grp * 4:12 + grp * 4].unsqueeze(2).to_broadcast([128, 4, 257]),
                                op=ALU.mult), r=[ck_, cnk_, "DECb"], w=["Cdb"])
                            V(lambda e: e.tensor_copy(
                                out=Qpad[grp][:, grp * 32:grp * 32 + 544].rearrange("p (b c) -> p b c", c=136)[:, :, 0:8],
                                in_=qT[:, 1024 + grp * 32:1056 + grp * 32].rearrange("p (b j) -> p b j", j=8)),
                              r=["qT%d" % p_, "Qpad%d" % grp], w=["Qpad%d" % grp])
                            for b_ in range(4):
                                last = (grp == 3 and b_ == 3)
                                MM(banks[bn_][:, 0:257], Qpad[grp][:, b_ * 128:(b_ + 1) * 128], Cdb[:, b_, 0:257], False, last,
                                   ["Qpad%d" % grp, "Cdb"], [bk(bn_)], last)
                            V(lambda e: e.tensor_tensor(
                                out=Kpad[:], in0=ktil[:, 8, :].unsqueeze(1).to_broadcast([128, 4, 128]),
                                in1=bm16[:, grp * 4:(grp + 1) * 4].unsqueeze(2).to_broadcast([128, 4, 128]), op=ALU.mult),
                              r=["ktil%d_8" % p_, "bm16"], w=["Kpad"])
                            for b_ in range(4):
                                bc = nb()
                                MM(banks[bc][:, 0:257], Kpad[:, b_, :], v1[:, 8, 0:257], True, True, ["Kpad"] + vk, [bk(bc)], True)
                                V(lambda e: e.scalar_tensor_tensor(
                                    out=Cst[:, b_, :], in0=Cst[:, b_, :], scalar=DECb[:, h, 8 + grp * 4 + b_:9 + grp * 4 + b_],
                                    in1=banks[bc][:, 0:257], op0=ALU.mult, op1=ALU.add),
                                  r=[ck_, cnk_, "Cdb", "DECb", bk(bc)], w=[ck_, cnk_])
                            P.dma("sp", lambda e: e.dma_start(
                                out=sCo_d[grp * 4:(grp + 1) * 4, h].rearrange("b k v -> k b v"), in_=Cst[:, :, 0:256]),
                                "CstO%d" % (r_ % 4), reads=[ck_, cnk_])
                            V(lambda e: e.tensor_copy(out=snout[:, grp * 4:(grp + 1) * 4, h:h + 1], in_=Cst[:, :, 256:257]),
                              r=[ck_, cnk_], w=["snout"])
                        out_stage(h, 8, bn_, 0)

                    for h in range(4):
                        proj(h)
                    nrot[0] = 3
                    for i_, _ in enumerate(zip(mloop(0), mloop(1), mloop(2), mloop(3))):
                        out_stage4(i_)
                    nrot[0] = 6
                    if has_s:
                        for h in range(4):
                            msample(h)
                    if has_s:
                        P.dma("sp", [lambda e: e.dma_start(out=pnT_d, in_=pnT[:]),
                                     lambda e: e.dma_start(out=snTo_d, in_=snout[:])], "nout", reads=["pnT", "snout"])
                    P.full_barrier()

                ck("p%d_heads" % ps_)
                if ps_ == 0:
                    V(lambda e: e.tensor_copy(out=xmtail[:], in_=xmT[:, :, 1025:1028]), r=xm_all, w=["xmtail"])
                with ExitStack() as sgt:
                    szt = [sbt(sgt, "szt%d" % i, [128, 512], BF16) for i in range(2)]
                    sot = [sbt(sgt, "sot%d" % i, [128, 512], BF16) for i in range(2)]
                    xst = [sbt(sgt, "xst%d" % i, [128, 512], F32) for i in range(2)]
                    t1t = [sbt(sgt, "t1t%d" % i, [128, 512], F32) for i in range(2)]
                    t2t = [sbt(sgt, "t2t%d" % i, [128, 512], F32) for i in range(2)]
                    cnt = 0
                    for fc in range(8):
                        slot = acquire_w()
                        h = fc // 2
                        for (c0, n) in tgs:
                            pr = cnt % 2
                            cnt += 1
                            bz, bo = nb(), nb()
                            for k in range(8):
                                MM(banks[bz][:, 0:n], wst[slot][:, k, 0:128], xnT[:, k, c0:c0 + n], k == 0, k == 7,
                                   ["wst%d" % slot] + xn_keys(c0, n), [bk(bz)], k == 7)
                            for k in range(8):
                                MM(banks[bo][:, 0:n], wst[slot][:, k, 128:256], xnT[:, k, c0:c0 + n], k == 0, k == 7,
                                   ["wst%d" % slot] + xn_keys(c0, n), [bk(bo)], k == 7)
                            A(lambda e: e.activation(out=szt[pr][:, 0:n], in_=banks[bz][:, 0:n], func=AF.Silu), r=[bk(bz)], w=["szt%d" % pr])
                            A(lambda e: e.activation(out=sot[pr][:, 0:n], in_=banks[bo][:, 0:n], func=AF.Tanh, scale=0.5), r=[bk(bo)],
                              w=["sot%d" % pr])
                            A(lambda e: e.activation(out=xst[pr][:, 0:n], in_=xcT[:, fc, c0:c0 + n], func=AF.Copy,
                                                     scale=mskcol[:, fc:fc + 1]), r=["xcT_%d" % fc, "consts"], w=["xst%d" % pr])
                            V(lambda e: e.scalar_tensor_tensor(
                                out=t1t[pr][:, 0:n], in0=sot[pr][:, 0:n], scalar=1.0, in1=mixA[:, fc, c0:c0 + n],
                                op0=ALU.add, op1=ALU.mult), r=["mixA_%d" % h, "sot%d" % pr], w=["t1t%d" % pr])
                            V(lambda e: e.scalar_tensor_tensor(
                                out=t2t[pr][:, 0:n], in0=t1t[pr][:, 0:n], scalar=mlnh[:, fc:fc + 1], in1=xst[pr][:, 0:n],
                                op0=ALU.mult, op1=ALU.add), r=["xst%d" % pr, "t1t%d" % pr, "mlnh"], w=["t2t%d" % pr])
                            G(lambda e: e.tensor_tensor(
                                out=mixA[:, fc, c0:c0 + n], in0=t2t[pr][:, 0:n], in1=szt[pr][:, 0:n], op=ALU.mult),
                              r=["t2t%d" % pr, "szt%d" % pr, "mixA_%d" % h], w=["mixA_%d" % h])
                    P.full_barrier()

            ck("p%d_ph2" % ps_)
            s34 = ExitStack()
            mixB = sbt(s34, "mixB", [128, 8, 1152], BF16)
            with ExitStack() as s3:
                vtok = sbt(s3, "vtok", [128, 9, 1024], BF16)
                Abig = sbt(s3, "Abig", [128, 4, 1152], F32)
                HB = []
                for p_ in range(2):
                    HB.append(dict(
                        QS=None, ZS=sbt(s3, "ZS%d" % p_, [128, 1152], BF16),
                        QT=sbt(s3, "QT%d" % p_, [128, 1152], BF16), KHAT=sbt(s3, "KHAT%d" % p_, [128, 1152], BF16),
                        KTT=None,
                        KA=sbt(s3, "KA%d" % p_, [128, 8, 128], BF16), KB=sbt(s3, "KB%d" % p_, [128, 8, 128], BF16),
                        DECH=sbt(s3, "DECH%d" % p_, [128, 32], F32),
                        Sbf=[sbt(s3, "Sbf%d_%d" % (p_, i), [128, 128], BF16) for i in range(2)],
                        KS=(sbt(s3, "KS%d" % p_, [128, 128], BF16) if has_s else None)))
                    G(lambda e: e.memset(HB[p_]["KA"][:], 0.0), w=["KA%d" % p_])
                    G(lambda e: e.memset(HB[p_]["KB"][:], 0.0), w=["KB%d" % p_])
                QS1 = sbt(s3, "QS", [128, 1152], BF16)
                KTT1 = sbt(s3, "KTT", [128, 1152], BF16)
                ATsb = [sbt(s3, "ATsb%d" % i, [128, 128], BF16) for i in range(2)]
                Ohs = [sbt(s3, "Oh%d" % i, [64, 16, 128], F32) for i in range(2)]
                SQ = sbt(s3, "SQ", [64, 2048], BF16)
                SQ2 = sbt(s3, "SQ2", [64, 2048], BF16)
                On = SQ2[:, :].rearrange("p (c v) -> p c v", v=128)
                ssh = sbt(s3, "ssh", [128, 20], F32)
                ssq = [sbt(s3, "ssq%d" % i, [64, 16], F32) for i in range(2)]
                junkO = sbt(s3, "junkO", [64, 128], BF16)
                if has_s:
                    Ssts = [sbt(s3, "Sst%d" % i, [128, 16, 128], F32) for i in range(2)]

                    def issue_sst(h_):
                        P.dma("sp", lambda e: e.dma_start(out=Ssts[h_ % 2][:], in_=sS_d[:, h_].rearrange("b k v -> k b v")),
                              "Sst%d" % (h_ % 2), writes=["Sst%d" % (h_ % 2)])
                    issue_sst(0)
                    Sb16 = sbt(s3, "Sb16", [128, 16, 128], BF16)
                    QpH = sbt(s3, "QpH", [128, 2176], BF16)
                    Kp16 = sbt(s3, "Kp16", [128, 16, 128], BF16)
                    tmpS = sbt(s3, "tmpS", [128, 4, 128], F32)
                    Ons = sbt(s3, "Ons", [128, 128], BF16)
                    G(lambda e: e.memset(QpH[:], 0.0), w=["QpH"])
                A1, A2, A3, A4 = Abig[:, 0, :], Abig[:, 1, :], Abig[:, 2, :], Abig[:, 3, :]
                for blk in range(2):
                    slot = acquire_w()
                    for i in range(NTp):
                        b = nb()
                        for k in range(8):
                            MM(banks[b][:, :], xnT[:, k, i * 128:(i + 1) * 128], wst[slot][:, k, :], k == 0, k == 7,
                               ["wst%d" % slot, "xnT_%d" % i], [bk(b)], k == 7)
                        A(lambda e: e.activation(func=AF.Copy, out=vtok[:, i, blk * 512:(blk + 1) * 512], in_=banks[b][:, :]),
                          r=[bk(b)], w=["vtok_%d" % i])

                def chain(h):
                    B = HB[h % 2]
                    p_ = h % 2
                    QS, ZS, QT, KHAT, KTT, KA, KB, DECH, KS = (B[k_] for k_ in ("QS", "ZS", "QT", "KHAT", "KTT", "KA", "KB", "DECH", "KS"))
                    QS, KTT = QS1, KTT1
                    sfx = "%d" % p_
                    slot = acquire_w()
                    for (c0, n) in tgs:
                        bs = [nb(), nb(), nb()]
                        for j in range(3):
                            for k in range(8):
                                MM(banks[bs[j]][:, 0:n], wst[slot][:, k, j * 128:(j + 1) * 128], xnT[:, k, c0:c0 + n], k == 0, k == 7,
                                   ["wst%d" % slot] + xn_keys(c0, n), [bk(bs[j])], k == 7)
                        A(lambda e: e.activation(out=A1[:, c0:c0 + n], in_=banks[bs[0]][:, 0:n], func=AF.Tanh, scale=0.5),
                          r=[bk(bs[0])], w=["A1"])
                        A(lambda e: e.activation(out=QS[:, c0:c0 + n], in_=banks[bs[1]][:, 0:n], func=AF.Silu),
                          r=[bk(bs[1])], w=["QS"])
                        A(lambda e: e.activation(out=ZS[:, c0:c0 + n], in_=banks[bs[2]][:, 0:n], func=AF.Silu),
                          r=[bk(bs[2])], w=["ZS" + sfx])
                        yield
                    A(lambda e: e.activation(out=A2[:, 0:W], in_=A1[:, 0:W], func=AF.Identity, scale=nhomlc[:, h:h + 1], bias=homlc[:, h:h + 1]),
                      r=["A1", "lbk"], w=["A2"])
                    yield
                    A(lambda e: e.activation(out=A1[:, 0:W], in_=A1[:, 0:W], func=AF.Ln, scale=homlc[:, h:h + 1], bias=lbhc[:, h:h + 1]),
                      r=["A1", "A2", "lbk"], w=["A1"])
                    yield
                    V(lambda e: e.tensor_tensor_scan(out=A3[:, 0:W], data0=msk64[:, 0:W], data1=A1[:, 0:W], initial=0.0,
                                                     op0=ALU.mult, op1=ALU.add), r=["A1", "msk64"], w=["A3"])
                    yield
                    A(lambda e: e.activation(out=A4[:, 0:W], in_=A3[:, 0:W], func=AF.Exp), r=["A3"], w=["A4"])
                    yield
                    G(lambda e: e.tensor_tensor(out=QT[:, 0:W], in0=QS[:, 0:W], in1=A4[:, 0:W], op=ALU.mult), r=["QS", "A4"], w=["QT" + sfx])
                    A(lambda e: e.activation(func=AF.Copy, out=DECH[:, 0:16].unsqueeze(2),
                                             in_=A4[:, 0:1024].rearrange("p (c t) -> p c t", t=64)[:, :, 63:64]), r=["A4"], w=["DECH" + sfx])
                    if has_s:
                        A(lambda e: e.activation(func=AF.Copy, out=DECH[:, 16:32].unsqueeze(2),
                                                 in_=A4[:, 1024:1152].rearrange("p (c t) -> p c t", t=8)[:, :, 7:8]),
                          r=["A4", "DECH" + sfx], w=["DECH" + sfx])
                    yield
                    A(lambda e: e.activation(out=A1[:, 0:W], in_=A3[:, 0:W], func=AF.Exp, scale=-1.0), r=["A3", "A1"], w=["A1"])
                    yield
                    G(lambda e: e.tensor_tensor(out=KHAT[:, 0:W], in0=A2[:, 0:W], in1=A1[:, 0:W], op=ALU.mult), r=["A2", "A1"], w=["KHAT" + sfx])
                    yield
                    G(lambda e: e.tensor_tensor(out=A1[:, 0:1024].rearrange("p (c t) -> p c t", t=64),
                                                in0=A1[:, 0:1024].rearrange("p (c t) -> p c t", t=64),
                                                in1=DECH[:, 0:16].unsqueeze(2).to_broadcast([128, 16, 64]), op=ALU.mult),
                      r=["A1", "DECH" + sfx, "KHAT" + sfx], w=["A1"])
                    if has_s:
                        G(lambda e: e.tensor_tensor(out=A1[:, 1024:1152].rearrange("p (c t) -> p c t", t=8),
                                                    in0=A1[:, 1024:1152].rearrange("p (c t) -> p c t", t=8),
                                                    in1=DECH[:, 16:32].unsqueeze(2).to_broadcast([128, 16, 8]), op=ALU.mult),
                          r=["A1", "DECH" + sfx, "KHAT" + sfx], w=["A1"])
                    yield
                    G(lambda e: e.tensor_tensor(out=KTT[:, 0:W], in0=A2[:, 0:W], in1=A1[:, 0:W], op=ALU.mult), r=["A2", "A1"], w=["KTT"])
                    yield
                    b = nb()
                    pinned.add(b)
                    for i in range(8):
                        TR(bbf(b)[:, i * 128:(i + 1) * 128], KTT[:, i * 128:(i + 1) * 128], identb[:], ["KTT", "identb"], [bk(b)], i == 7)
                    A(lambda e: e.activation(func=AF.Copy, out=KA[0:64, :, :], in_=bbf(b)[0:64, :].rearrange("p (i k) -> p i k", i=8)),
                      r=[bk(b)], w=["KA" + sfx])
                    yield
                    A(lambda e: e.activation(func=AF.Copy, out=KB[64:128, :, :], in_=bbf(b)[64:128, :].rearrange("p (i k) -> p i k", i=8)),
                      r=[bk(b)], w=["KB" + sfx])
                    pinned.discard(b)
                    if has_s:
                        b2 = nb()
                        TR(bbf(b2)[:, 0:128], KTT[:, 1024:1152], identb[:], ["KTT", "identb"], [bk(b2)], True)
                        A(lambda e: e.activation(func=AF.Copy, out=KS[:], in_=bbf(b2)[:, 0:128]), r=[bk(b2)], w=["KS" + sfx])
                    yield

                def loop(h):
                    B = HB[h % 2]
                    p_ = h % 2
                    QT, KHAT, KA, KB, DECH, Sbf = B["QT"], B["KHAT"], B["KA"], B["KB"], B["DECH"], B["Sbf"]
                    sfx = "%d" % p_
                    sfk = "SF%d" % h
                    hc = slice(h * 128, (h + 1) * 128)
                    sbk = lambda c_: "Sbf%d_%d" % (p_, c_)
                    cur = 0
                    if ps_ > 0:
                        A(lambda e: e.activation(func=AF.Copy, out=Sbf[0][:], in_=SF[:, h, :]), r=[sfk], w=[sbk(0)])

                    def indep(i):
                        cs = slice(i * 128, (i + 1) * 128)
                        par = i % 2
                        ba = nb()
                        MM(banks[ba][:, 0:128], KHAT[:, cs], QT[:, cs], True, True, ["KHAT" + sfx, "QT" + sfx], [bk(ba)], True)
                        V(lambda e: e.tensor_tensor(out=ATsb[par][:], in0=banks[ba][:, 0:128], in1=mask2[:], op=ALU.mult),
                          r=[bk(ba), "mask2"], w=["ATsb%d" % par])
                        bks = []
                        for KX, kxk in ((KA, "KA" + sfx), (KB, "KB" + sfx)):
                            bc = nb()
                            pinned.add(bc)
                            MM(banks[bc][:, 0:128], KX[:, i, :], vtok[:, i, hc], True, True, [kxk, "vtok_%d" % i], [bk(bc)], True)
                            bks.append(bc)
                        return bks

                    pre = indep(0)
                    for i in range(8):
                        gi = ps_ * 8 + i
                        par = i % 2
                        mine = pre
                        if i < 7:
                            pre = indep(i + 1)
                        bo = nb()
                        first = (gi == 0)
                        for half in range(2):
                            co = slice(half * 128, (half + 1) * 128)
                            qs_ = slice(i * 128 + half * 64, i * 128 + (half + 1) * 64)
                            skip = first and half == 0
                            MM(banks[bo][0:64, co], ATsb[par][:, half * 64:(half + 1) * 64], vtok[:, i, hc], True, skip,
                               ["ATsb%d" % par, "vtok_%d" % i], [bk(bo)], skip and False)
                            if not skip:
                                MM(banks[bo][0:64, co], QT[:, qs_], Sbf[cur][:], False, True, ["QT" + sfx, sbk(cur)], [bk(bo)], half == 1)
                            bc = mine[half]
                            nxt = 1 - cur
                            dcol = 2 * i + half
                            V(lambda e: e.scalar_tensor_tensor(out=Sbf[nxt][:], in0=SF[:, h, :], scalar=DECH[:, dcol:dcol + 1],
                                                               in1=banks[bc][:, 0:128], op0=ALU.mult, op1=ALU.add),
                              r=[sfk, "DECH" + sfx, bk(bc)], w=[sbk(nxt)])
                            V(lambda e: e.scalar_tensor_tensor(out=SF[:, h, :], in0=SF[:, h, :], scalar=DECH[:, dcol:dcol + 1],
                                                               in1=banks[bc][:, 0:128], op0=ALU.mult, op1=ALU.add),
                              r=[sfk, "DECH" + sfx, bk(bc)], w=[sfk])
                            pinned.discard(bc)
                            cur = nxt
                        A(lambda e: e.activation(func=AF.Copy, out=Ohs[p_][:, 2 * i:2 * i + 2, :],
                                                 in_=banks[bo][0:64, 0:256].rearrange("p (c v) -> p c v", c=2)), r=[bk(bo)], w=["Oh%d" % p_])
                        for half in range(2):
                            A(lambda e: e.activation(func=AF.Square, out=SQ[:, (2 * i + half) * 128:(2 * i + half + 1) * 128],
                                                     in_=Ohs[p_][:, 2 * i + half, :],
                                                     accum_out=ssq[p_][:, 2 * i + half:2 * i + half + 1]),
                              r=["Oh%d" % p_], w=["SQ_%d" % (2 * i + half), "ssq%d" % p_])
                        yield

                def postg(h):
                    B = HB[h % 2]
                    p_ = h % 2
                    ZS = B["ZS"]
                    Oh = Ohs[p_]
                    ohk = "Oh%d" % p_
                    sfx = "%d" % p_
                    A(lambda e: e.activation(out=ssh[0:64, 0:16], in_=ssq[p_][:, :], func=AF.Ln, scale=1.0 / 128.0, bias=epsc[0:64, 0:1]),
                      r=["ssq%d" % p_, "epsc"], w=["ssh"])
                    A(lambda e: e.activation(out=ssh[0:64, 0:16], in_=ssh[0:64, 0:16], func=AF.Exp, scale=-0.5), r=["ssh"], w=["ssh"])
                    yield
                    G(lambda e: e.tensor_tensor(out=On, in0=Oh[:], in1=ssh[0:64, 0:16].unsqueeze(2).to_broadcast([64, 16, 128]), op=ALU.mult),
                      r=[ohk, "ssh"], w=["On"])
                    yield
                    b = nb()
                    pinned.add(b)
                    for c in range(16):
                        TR(bbf(b)[:, c * 64:(c + 1) * 64], On[0:64, c, :], identb[0:64, 0:64], ["On", "identb"], [bk(b)], c == 15)
                    yield
                    V(lambda e: e.scalar_tensor_tensor(out=mixB[:, h, 0:1024], in0=bbf(b)[:, 0:1024], scalar=hncol[:, h:h + 1],
                                                       in1=ZS[:, 0:1024], op0=ALU.mult, op1=ALU.mult),
                      r=[bk(b), "consts", "ZS" + sfx], w=["mixB_%d" % h])
                    pinned.discard(b)
                    yield

                def post_sample(h):
                    B = HB[h % 2]
                    p_ = h % 2
                    QT, KHAT, ZS, DECH, KS = B["QT"], B["KHAT"], B["ZS"], B["DECH"], B["KS"]
                    sfx = "%d" % p_
                    sfk = "SF%d" % h
                    hc = slice(h * 128, (h + 1) * 128)
                    P.dma("sp", lambda e: e.dma_start(out=pS_d[h], in_=SF[:, h, :]), "pS", reads=[sfk])
                    cs = slice(1024, 1152)
                    ba = nb()
                    MM(banks[ba][:, 0:128], KHAT[:, cs], QT[:, cs], True, True, ["KHAT" + sfx, "QT" + sfx], [bk(ba)], True)
                    V(lambda e: e.tensor_tensor(out=ATsb[0][:], in0=banks[ba][:, 0:128], in1=bmask[:], op=ALU.mult),
                      r=[bk(ba), "bmask"], w=["ATsb0"])
                    if h + 1 < 8:
                        issue_sst(h + 1)
                    Sst = Ssts[h % 2]
                    sstk = "Sst%d" % (h % 2)
                    A(lambda e: e.activation(func=AF.Copy, out=Sb16[:], in_=Sst[:]), r=[sstk], w=["Sb16"])
                    V(lambda e: e.tensor_copy(out=QpH[:].rearrange("p (b c) -> p b c", c=136)[:, :, 0:8],
                                              in_=QT[:, 1024:1152].rearrange("p (b j) -> p b j", j=8)), r=["QT" + sfx, "QpH"], w=["QpH"])
                    bo = RES
                    MM(banks[bo][:, 0:128], ATsb[0][:], vtok[:, 8, hc], True, False, ["ATsb0", "vtok_8"], [bk(bo)], False)
                    for b_ in range(16):
                        MM(banks[bo][:, 0:128], QpH[:, b_ * 128:(b_ + 1) * 128], Sb16[:, b_, :], False, b_ == 15, ["QpH", "Sb16"], [bk(bo)], b_ == 15)
                    V(lambda e: e.tensor_tensor(out=Kp16[:], in0=KS[:, :].unsqueeze(1).to_broadcast([128, 16, 128]),
                                                in1=bm16[:, :].unsqueeze(2).to_broadcast([128, 16, 128]), op=ALU.mult),
                      r=["KS" + sfx, "bm16"], w=["Kp16"])
                    for bq in range(4):
                        bc = nb()
                        for j in range(4):
                            MM(banks[bc][:, j * 128:(j + 1) * 128], Kp16[:, 4 * bq + j, :], vtok[:, 8, hc], True, True, ["Kp16", "vtok_8"],
                               [bk(bc)], j == 3)
                        G(lambda e: e.tensor_tensor(out=tmpS[:], in0=Sst[:, 4 * bq:4 * bq + 4, :],
                                                    in1=DECH[:, 16 + 4 * bq:20 + 4 * bq].unsqueeze(2).to_broadcast([128, 4, 128]),
                                                    op=ALU.mult), r=[sstk, "Sb16", "DECH" + sfx], w=["tmpS"])
                        V(lambda e: e.tensor_tensor(out=Sst[:, 4 * bq:4 * bq + 4, :], in0=tmpS[:],
                                                    in1=banks[bc][:, :].rearrange("p (j v) -> p j v", j=4), op=ALU.add),
                          r=["tmpS", bk(bc), "Sb16"], w=[sstk])
                    P.dma("sp", lambda e: e.dma_start(out=sSo_d[:, h].rearrange("b k v -> k b v"), in_=Sst[:]), "SstO%d" % (h % 2), reads=[sstk])
                    A(lambda e: e.activation(out=Ons[:], in_=banks[bo][:, 0:128], func=AF.Square, accum_out=ssh[:, 16:17]),
                      r=[bk(bo)], w=["Ons", "ssh2"])
                    A(lambda e: e.activation(out=ssh[:, 17:18], in_=ssh[:, 16:17], func=AF.Ln, scale=1.0 / 128.0, bias=epsc[:, 0:1]),
                      r=["ssh2", "epsc"], w=["ssh3"])
                    A(lambda e: e.activation(out=ssh[:, 17:18], in_=ssh[:, 17:18], func=AF.Exp, scale=-0.5), r=["ssh3"], w=["ssh3"])
                    V(lambda e: e.tensor_scalar(out=Ons[:], in0=banks[bo][:, 0:128], scalar1=ssh[:, 17:18], scalar2=None, op0=ALU.mult),
                      r=[bk(bo), "ssh3", "Ons"], w=["Ons"])
                    b = nb()
                    TR(bbf(b)[:, 0:128], Ons[:], identb[:], ["Ons", "identb"], [bk(b)], True)
                    V(lambda e: e.scalar_tensor_tensor(out=mixB[:, h, 1024:1152], in0=bbf(b)[:, 0:128], scalar=hncol[:, h:h + 1],
                                                       in1=ZS[:, 1024:1152], op0=ALU.mult, op1=ALU.mult),
                      r=[bk(b), "consts", "ZS" + sfx], w=["mixB_%d" % h])

                nrot[0] = 7
                for _ in chain(0):
                    pass
                gp = None
                for h in range(8):
                    gc = chain(h + 1) if h < 7 else None
                    for _ in loop(h):
                        if gp is not None:
                            for _k in range(2):
                                if next(gp, "done") == "done":
                                    gp = None
                                    break
                        elif gc is not None:
                            for _k in range(2):
                                if next(gc, "done") == "done":
                                    gc = None
                                    break
                    if gp is not None:
                        for _ in gp:
                            pass
                    if gc is not None:
                        for _ in gc:
                            pass
                    if has_s:
                        post_sample(h)
                    gp = postg(h)
                for _ in gp:
                    pass
                nrot[0] = 6
                P.full_barrier()

            ck("p%d_ph3" % ps_)
            with ExitStack() as s4:
                wout = sbt(s4, "wout", [128, 16, 1024], BF16)
                gfin = sbt(s4, "gfin", [128, 1024], F32)
                P.dma("sp", lambda e: e.dma_start(out=gfin[:], in_=gfin_d), "gfin", writes=["gfin"])
                xr = [sbt(s4, "xr%d" % i, [128, 1024], F32) for i in range(3)]
                yt = [sbt(s4, "yt%d" % i, [128, 1024], F32) for i in range(2)]
                junk4 = sbt(s4, "junk4", [128, 1024], BF16)
                ss4 = sbt(s4, "ss4", [128, 9], F32)
                for q in range(8):
                    P.dma("pool", lambda e, q=q: e.dma_start(out=wout[:, 2 * q:2 * q + 2, :],
                                                             in_=w_out_d[q * 256:(q + 1) * 256, :].rearrange("(k p) n -> p k n", p=128)),
                          "wout%d" % q, writes=["wout%d" % q])
                mix_keys = ["mixA_%d" % h for h in range(4)] + ["mixB_%d" % h for h in range(8)]
                for i in range(NTp):
                    sl = i % 2
                    xs_ = i % 3
                    P.dma("sp", lambda e, i=i, xs_=xs_: e.dma_start(out=xr[xs_][:], in_=x_d[gt(i)]), "xr%d" % xs_, writes=["xr%d" % xs_])
                    bs = [nb(), nb()]
                    for hf in range(2):
                        for kc in range(16):
                            src = mixA if kc < 8 else mixB
                            MM(banks[bs[hf]][:, :], src[:, kc % 8, i * 128:(i + 1) * 128], wout[:, kc, hf * 512:(hf + 1) * 512], kc == 0, kc == 15,
                               mix_keys + ["wout%d" % (kc // 2)], [bk(bs[hf])], kc == 15)
                        V(lambda e, hf=hf, sl=sl, b=bs[hf]: e.tensor_tensor(out=yt[sl][:, hf * 512:(hf + 1) * 512], in0=banks[b][:, :],
                                                                            in1=xr[xs_][:, hf * 512:(hf + 1) * 512], op=ALU.add),
                          r=[bk(bs[hf]), "xr%d" % xs_], w=["yt%d_%d" % (sl, hf)])
                    A(lambda e, i=i, sl=sl: e.activation(out=junk4[:], in_=yt[sl][:], func=AF.Square, accum_out=ss4[:, i:i + 1]),
                      r=["yt%d_0" % sl, "yt%d_1" % sl], w=["junk4", "ss4_%d" % i])
                    A(lambda e, i=i: e.activation(out=ss4[:, i:i + 1], in_=ss4[:, i:i + 1], func=AF.Ln, scale=1.0 / 1024.0, bias=epsc[:, 0:1]),
                      r=["ss4_%d" % i, "epsc"], w=["ss4_%d" % i])
                    A(lambda e, i=i: e.activation(out=ss4[:, i:i + 1], in_=ss4[:, i:i + 1], func=AF.Exp, scale=-0.5),
                      r=["ss4_%d" % i], w=["ss4_%d" % i])
                    V(lambda e, i=i, sl=sl: e.scalar_tensor_tensor(out=yt[sl][:], in0=yt[sl][:], scalar=ss4[:, i:i + 1], in1=gfin[:],
                                                                   op0=ALU.mult, op1=ALU.mult),
                      r=["yt%d_0" % sl, "yt%d_1" % sl, "ss4_%d" % i, "gfin"], w=["yt%d_0" % sl, "yt%d_1" % sl])
                    P.dma("pool", lambda e, i=i, sl=sl: e.dma_start(out=y_d[gt(i)], in_=yt[sl][:]), "yo%d" % sl,
                          reads=["yt%d_0" % sl, "yt%d_1" % sl])
                P.full_barrier()
                ck("p%d_ph4" % ps_)
                P.final_wait("sp") if ps_ == 1 else None
            s34.close()

        if DBG_PRINT:
            print("SBUF peak bytes/partition:", peak[0], [(n, b) for n, b in peak_names[:40]])
        with nc.Block() as block:
            P.emit(block)
    return nc


_NC_CACHE = {}


def kernel(x_prompt, x_sample, state_mlstm_conv, state_mlstm_C, state_mlstm_n, state_mlstm_m,
           state_hgrn_S, g_norm, w_in, conv_w, conv_b, w_q, w_k, w_v, w_gate, b_gate, m_ln,
           m_skip, lb_param, h_norm, w_out, g_final):
    f = lambda a: np.ascontiguousarray(np.asarray(a, dtype=np.float32))
    x_prompt, x_sample = f(x_prompt), f(x_sample)
    col = lambda v: f(np.asarray(v, np.float32).reshape(8, 128).T)
    shared = {
        "gcol": col(g_norm[0]),
        "w_in": f(w_in[0]),
        "cwcol": f(np.asarray(conv_w[0], np.float32).reshape(4, 8, 128).transpose(2, 1, 0)),
        "cbcol": col(conv_b[0]),
        "w_q": f(w_q[0]), "w_k": f(w_k[0]), "w_v": f(w_v[0]),
        "wqT": f(np.asarray(w_q[0], np.float32).transpose(2, 0, 1)),
        "wkT": f(np.asarray(w_k[0], np.float32).transpose(2, 0, 1)),
        "wvT": f(np.asarray(w_v[0], np.float32).transpose(0, 2, 1).reshape(4, 2, 128, 256).transpose(2, 0, 1, 3)),
        "wg": f(np.asarray(w_gate[0], np.float32).reshape(16, 128, 8).transpose(1, 0, 2)),
        "bg": f(np.asarray(b_gate[0], np.float32).reshape(2, 4).T),
        "mlncol": col(m_ln[0]),
        "mskipcol": col(m_skip[0]),
        "lbp": f(np.asarray(lb_param, np.float32).reshape(2, 8, 128).transpose(2, 0, 1)),
        "hnormcol": col(h_norm[0]),
        "w_out": f(w_out[0]),
        "gfin": f(np.broadcast_to(np.asarray(g_final, np.float32)[None, :], (128, 1024))),
    }
    in_maps = []
    for c in range(8):
        sl = slice(16 * c, 16 * c + 16)
        xa = np.concatenate([x_prompt[c].reshape(16, 128, 1024), x_sample[sl].reshape(1, 128, 1024)], 0)
        m = dict(shared)
        m["x"] = f(xa)
        m["sconv"] = f(np.asarray(state_mlstm_conv[0, sl], np.float32).reshape(48, 1024))
        m["sC"] = f(state_mlstm_C[0, sl])
        m["snT"] = f(np.asarray(state_mlstm_n[0, sl], np.float32).transpose(2, 0, 1))
        m["m0T"] = f(np.asarray(state_mlstm_m[0, sl], np.float32).T)
        m["sS"] = f(state_hgrn_S[0, sl])
        in_maps.append(m)
    if "nc" not in _NC_CACHE:
        _NC_CACHE["nc"] = build_program()
    nc = _NC_CACHE["nc"]
    res = run_bass_kernel_spmd(nc, in_maps, core_ids=list(range(8)))
    R = res.results
    y_prompt = np.stack([R[c]["y"][0:16].reshape(2048, 1024) for c in range(8)], 0)
    y_sample = np.concatenate([R[c]["y"][16].reshape(16, 8, 1024) for c in range(8)], 0)
    p_conv = np.stack([R[c]["pconv"] for c in range(8)], 0)[None]
    p_C = np.stack([R[c]["pC"] for c in range(8)], 0)[None]
    p_n = np.stack([R[c]["pnT"].T for c in range(8)], 0)[None]
    p_m = np.stack([R[c]["pmT"][:, 0] for c in range(8)], 0)[None]
    p_S = np.stack([R[c]["pS"] for c in range(8)], 0)[None]
    s_conv = np.concatenate([R[c]["sconvo"] for c in range(8)], 0)[None]
    s_C = np.concatenate([R[c]["sCo"] for c in range(8)], 0)[None]
    s_n = np.concatenate([R[c]["snTo"].transpose(1, 2, 0) for c in range(8)], 0)[None]
    s_m = np.concatenate([R[c]["smTo"].T for c in range(8)], 0)[None]
    s_S = np.concatenate([R[c]["sSo"] for c in range(8)], 0)[None]
    outs = (y_prompt, y_sample, p_conv, p_C, p_n, p_m, p_S, s_conv, s_C, s_n, s_m, s_S)
    return tuple(np.ascontiguousarray(o, dtype=np.float32) for o in outs)
```

```python
import numpy as np
from contextlib import ExitStack
import concourse.bass as bass
import concourse.mybir as mybir
from concourse.bass_utils import run_bass_kernel_spmd

F32 = mybir.dt.float32
BF16 = mybir.dt.bfloat16
AF = mybir.ActivationFunctionType
ALU = mybir.AluOpType
AX = mybir.AxisListType
EPS = 1e-6


class _Rec:
    def __getattr__(self, name):
        def call(*a, **k):
            self.cap = (name, a, k)
            return self
        return call


def _capture(fn):
    r = _Rec()
    fn(r)
    name, a, k = r.cap
    f = lambda e: getattr(e, name)(*a, **k)
    f.desc = (name, k.get("out", a[0] if a else None))
    return f


class Prog:
    ENGS = ("pe", "act", "dve", "pool", "sp")
    CE = ("pe", "act", "dve", "pool")

    def __init__(self, nc, stack):
        self.nc = nc
        self.q = {e: [] for e in self.ENGS}
        self.n = {e: 0 for e in self.ENGS}
        self.incflag = {e: [None] for e in self.ENGS}
        self.seen = {e: {} for e in self.ENGS}
        self.lastw = {}
        self.readers = {}
        self.dsem = {}
        self.dcnt = {}
        self._stack = stack
        self.bar = {}
        self.stopped = False
        self.bar_tile = stack.enter_context(nc.sbuf_tensor("bar_tile", [128, 2], F32))
        self.esem = {e: stack.enter_context(nc.semaphore("sem_" + e)) for e in self.CE}

    def _slot(self, name):
        if name not in self.dsem:
            self.dsem[name] = self._stack.enter_context(self.nc.semaphore("dma_" + name))
            self.dcnt[name] = 0
        return self.dsem[name]

    def _prune(self, eng, tickets):
        best = {}
        for t in tickets:
            key = t[1] if t[0] == "eng" else id(t[1])
            if key not in best or best[key][2] < t[2]:
                best[key] = t
        waits = []
        for key, t in best.items():
            if self.seen[eng].get(key, 0) >= t[2]:
                continue
            self.seen[eng][key] = t[2]
            waits.append(t)
        return waits

    def _deps(self, eng, reads, writes):
        deps = []
        for k in reads:
            w = self.lastw.get(k)
            if w is not None:
                deps.append((w, "raw"))
        for k in writes:
            w = self.lastw.get(k)
            if w is not None:
                deps.append((w, "waw"))
            for r in self.readers.get(k, ()):
                deps.append((r, "war"))
        bt = self.bar.get(eng)
        if bt is not None:
            self.bar[eng] = None
            deps.append((bt, "raw"))
        tickets = []
        for (t, kind) in deps:
            if t[0] == "eng" and t[1] == eng:
                if eng == "pe" or kind != "raw":
                    continue
            tickets.append(t)
        return self._prune(eng, tickets)

    def _commit(self, ticket, reads, writes):
        for k in reads:
            self.readers.setdefault(k, []).append(ticket)
        for k in writes:
            self.lastw[k] = ticket
            self.readers[k] = []

    def _all_outstanding(self, skip_slots=()):
        ts = []
        for e in self.CE:
            if self.n[e] > 0:
                ts.append(("eng", e, self.n[e]))
        for s_, sem in self.dsem.items():
            if s_ in skip_slots:
                continue
            ts.append(("dma", sem, 16 * self.dcnt[s_]))
        return ts

    def full_barrier(self):
        if self.stopped:
            return
        ts = [t for t in self._all_outstanding(("wst0", "wst1", "pC", "pS")) if not (t[0] == "eng" and t[1] == "dve")]
        waits = self._prune("dve", ts)
        self.n["dve"] += 1
        self.incflag["dve"].append(True)
        ticket = ("eng", "dve", self.n["dve"])
        fn = _capture(lambda e: e.memset(self.bar_tile[:, 0:1], 0.0))
        self.q["dve"].append((waits, fn, ticket))
        for e in self.ENGS:
            if e != "dve":
                self.bar[e] = ticket

    def op(self, eng, fn, reads=(), writes=(), inc=True):
        if self.stopped:
            return
        fn = _capture(fn)
        waits = self._deps(eng, reads, writes)
        self.n[eng] += 1
        self.incflag[eng].append(bool(inc))
        ticket = ("eng", eng, self.n[eng])
        self.q[eng].append((waits, fn, ticket))
        self._commit(ticket, reads, writes)

    def dma(self, eng, fns, slot, reads=(), writes=()):
        if self.stopped:
            return
        if not isinstance(fns, (list, tuple)):
            fns = [fns]
        fns = [_capture(f_) for f_ in fns]
        sem = self._slot(slot)
        waits = self._deps(eng, reads, writes)
        prev = self.dcnt[slot]
        if prev > 0:
            waits += self._prune(eng, [("dma", sem, 16 * prev)])
        self.dcnt[slot] += len(fns)
        ticket = ("dma", sem, 16 * self.dcnt[slot])
        for i, fn in enumerate(fns):
            self.q[eng].append((waits if i == 0 else [], fn, ("dmainc", sem)))
        self._commit(ticket, reads, writes)

    def final_wait(self, eng="sp"):
        self.q[eng].append((self._all_outstanding(), None, None))

    def emit(self, block):
        q = self.q
        nxt = {}
        for e in self.CE:
            fl = self.incflag[e]
            r = [0] * (len(fl) + 1)
            last = None
            for j in range(len(fl) - 1, 0, -1):
                if fl[j]:
                    last = j
                r[j] = last
            nxt[e] = r
        needed = {e: set() for e in self.CE}
        for e in self.ENGS:
            for waits, fn, tk in q[e]:
                for t in waits:
                    if t[0] == "eng":
                        j = nxt[t[1]][t[2]]
                        assert j is not None, ("dependency on a trailing non-signalling instruction", t)
                        needed[t[1]].add(j)
        if EAGER_SIGNALS:
            for e in self.CE:
                needed[e] = {j for j in range(1, len(self.incflag[e])) if self.incflag[e][j]}
        rank = {}
        for e in self.CE:
            rk = {}
            for c, j in enumerate(sorted(needed[e])):
                rk[j] = c + 1
            rank[e] = rk
        esem = self.esem

        def run(e, lst):
            for waits, fn, tk in lst:
                for t in waits:
                    if t[0] == "eng":
                        e.wait_ge(esem[t[1]], rank[t[1]][nxt[t[1]][t[2]]])
                    else:
                        e.wait_ge(t[1], t[2])
                if fn is None:
                    continue
                ins = fn(e)
                if tk is None:
                    continue
                if tk[0] == "dmainc":
                    ins.then_inc(tk[1], 16)
                elif tk[2] in needed[tk[1]]:
                    ins.then_inc(esem[tk[1]], 1)

        @block.sync
        def _(e):
            run(e, q["sp"])

        @block.gpsimd
        def _(e):
            run(e, q["pool"])

        @block.scalar
        def _(e):
            run(e, q["act"])

        @block.vector
        def _(e):
            run(e, q["dve"])

        @block.tensor
        def _(e):
            run(e, q["pe"])


class _Stop(Exception):
    pass


DBG_STOP = None
DBG_PRINT = False
EAGER_SIGNALS = True


def build_program():
    nc = bass.Bass("TRN2", target_bir_lowering=False)

    def ck(tag):
        if DBG_STOP == tag:
            P_holder[0].stopped = True

    P_holder = [None]

    def din(name, shape):
        return nc.dram_tensor(name, list(shape), F32, kind="ExternalInput").ap()

    def dout(name, shape):
        return nc.dram_tensor(name, list(shape), F32, kind="ExternalOutput").ap()

    x_d = din("x", [17, 128, 1024])
    sconv_d = din("sconv", [48, 1024])
    sC_d = din("sC", [16, 4, 128, 256])
    snT_d = din("snT", [128, 16, 4])
    m0T_d = din("m0T", [4, 16])
    sS_d = din("sS", [16, 8, 128, 128])
    gcol_d = din("gcol", [128, 8])
    w_in_d = din("w_in", [1024, 7168])
    cw_d = din("cwcol", [128, 8, 4])
    cb_d = din("cbcol", [128, 8])
    wq_d = din("w_q", [4, 256, 128])
    wk_d = din("w_k", [4, 256, 128])
    wv_d = din("w_v", [4, 256, 256])
    wqT_d = din("wqT", [128, 4, 256])
    wkT_d = din("wkT", [128, 4, 256])
    wvT_d = din("wvT", [128, 4, 2, 256])
    wg_d = din("wg", [128, 16, 8])
    bg_d = din("bg", [4, 2])
    mln_d = din("mlncol", [128, 8])
    msk_d = din("mskipcol", [128, 8])
    lbp_d = din("lbp", [128, 2, 8])
    hn_d = din("hnormcol", [128, 8])
    w_out_d = din("w_out", [2048, 1024])
    gfin_d = din("gfin", [128, 1024])

    y_d = dout("y", [17, 128, 1024])
    pconv_d = dout("pconv", [3, 1024])
    pC_d = dout("pC", [4, 128, 256])
    pnT_d = dout("pnT", [128, 4])
    pmT_d = dout("pmT", [4, 1])
    pS_d = dout("pS", [8, 128, 128])
    sconvo_d = dout("sconvo", [16, 3, 1024])
    sCo_d = dout("sCo", [16, 4, 128, 256])
    snTo_d = dout("snTo", [128, 16, 4])
    smTo_d = dout("smTo", [4, 16])
    sSo_d = dout("sSo", [16, 8, 128, 128])

    with ExitStack() as st:
        P = Prog(nc, st)
        P_holder[0] = P

        uniq = [0]
        peak = [0]
        peak_names = []

        live = {}

        def sbt(stack, name, shape, dt):
            uniq[0] += 1
            nbytes = int(np.prod(shape[1:])) * (4 if dt == F32 else 2)
            key = uniq[0]
            live[key] = (name, nbytes)
            stack.callback(lambda: live.pop(key))
            tot = sum(v[1] for v in live.values())
            if tot > peak[0]:
                peak[0] = tot
                peak_names[:] = sorted(live.values(), key=lambda t: -t[1])
            return stack.enter_context(nc.sbuf_tensor("s%d_%s" % (uniq[0], name), list(shape), dt))

        banks = [st.enter_context(nc.psum_tensor(f"bank{i}", [128, 512], F32)) for i in range(8)]
        bank_ctr = [0]

        nrot = [6]

        pinned = set()

        def nb():
            for _t in range(16):
                i = bank_ctr[0] % nrot[0]
                bank_ctr[0] += 1
                if i not in pinned:
                    return i
            raise RuntimeError("no free PSUM bank")
        RES = 7

        def bk(i):
            return "bank%d" % i

        def bbf(i):
            return banks[i][:].bitcast(BF16)

        V = lambda fn, r=(), w=(): P.op("dve", fn, reads=r, writes=w)
        A = lambda fn, r=(), w=(): P.op("act", fn, reads=r, writes=w)
        G = lambda fn, r=(), w=(): P.op("pool", fn, reads=r, writes=w)

        def MM(out, lhsT, rhs, start, stop, r, w, inc):
            P.op("pe", lambda e: e.matmul(out, lhsT=lhsT, rhs=rhs, start=start, stop=stop), reads=r, writes=w, inc=inc)

        def TR(out, in_, ident, r, w, inc):
            P.op("pe", lambda e: e.transpose(out=out, in_=in_, identity=ident), reads=r, writes=w, inc=inc)

        identf = sbt(st, "identf", [128, 128], F32)
        identb = sbt(st, "identb", [128, 128], BF16)
        onesf = sbt(st, "onesf", [128, 128], F32)
        cmask = sbt(st, "cmask", [128, 128], BF16)
        mask2 = sbt(st, "mask2", [128, 128], BF16)
        bmask = sbt(st, "bmask", [128, 128], BF16)
        bm16f = sbt(st, "bm16f", [128, 16], F32)
        bm16 = sbt(st, "bm16", [128, 16], BF16)
        msk64 = sbt(st, "msk64", [128, 1152], BF16)
        gcol = sbt(st, "gcol", [128, 8], F32)
        cwcol = sbt(st, "cwcol", [128, 8, 4], F32)
        cbcol = sbt(st, "cbcol", [128, 8], F32)
        mlncol = sbt(st, "mlncol", [128, 8], F32)
        mskcol = sbt(st, "mskcol", [128, 8], F32)
        hncol = sbt(st, "hncol", [128, 8], F32)
        lbp = sbt(st, "lbp", [128, 2, 8], F32)
        lbc = sbt(st, "lbc", [128, 8], F32)
        omlc = sbt(st, "omlc", [128, 8], F32)
        nomlc = sbt(st, "nomlc", [128, 8], F32)
        homlc = sbt(st, "homlc", [128, 8], F32)
        nhomlc = sbt(st, "nhomlc", [128, 8], F32)
        lbhc = sbt(st, "lbhc", [128, 8], F32)
        mlnh = sbt(st, "mlnh", [128, 8], F32)
        bg = sbt(st, "bg", [4, 2], F32)
        nbgf = sbt(st, "nbgf", [4, 1], F32)
        one4 = sbt(st, "one4", [128, 1], F32)
        epsc = sbt(st, "epsc", [128, 1], F32)
        m0T = sbt(st, "m0T", [4, 16], F32)
        ginit = sbt(st, "ginit", [4, 1], F32)
        minit = sbt(st, "minit", [4, 1], F32)
        GI = sbt(st, "GI", [128, 16, 4], BF16)
        GF = sbt(st, "GF", [128, 16, 4], BF16)
        xnT = sbt(st, "xnT", [128, 8, 1152], BF16)
        mixA = sbt(st, "mixA", [128, 8, 1152], BF16)
        wst = [sbt(st, "wst%d" % i, [128, 8, 512], BF16) for i in range(2)]
        CF = sbt(st, "CF", [128, 4, 257], F32)
        SF = sbt(st, "SF", [128, 8, 128], F32)
        pnT = sbt(st, "pnT", [128, 4], F32)
        snout = sbt(st, "snout", [128, 16, 4], F32)
        nst = sbt(st, "nst", [128, 16, 4], F32)
        tokS = sbt(st, "tokS", [128, 9, 2, 4], F32)
        xmtail = sbt(st, "xmtail", [128, 8, 3], BF16)
        DECb = sbt(st, "DECb", [128, 4, 24], F32)

        wsched = []
        for _p in range(2):
            wsched += [[(0, 512, 0)], [(512, 512, 0)]]
            wsched += [[(1024 + fc * 128, 128, 0), (2048 + fc * 128, 128, 128)] for fc in range(8)]
            wsched += [[(5120, 512, 0)], [(5632, 512, 0)]]
            wsched += [[(3072 + h * 128, 128, 0), (4096 + h * 128, 128, 128), (6144 + h * 128, 128, 256)] for h in range(8)]
        w_issued = [0]
        w_next = [0]

        def issue_w(k):
            sl_ = k % 2
            fns = []
            for (c0, n, d0) in wsched[k]:
                fns.append(lambda e, c0=c0, n=n, d0=d0: e.dma_start(
                    out=wst[sl_][:, :, d0:d0 + n],
                    in_=w_in_d[:, c0:c0 + n].rearrange("(k p) n -> p k n", p=128)))
            P.dma("pool", fns, "wst%d" % sl_, writes=["wst%d" % sl_])

        def acquire_w():
            k = w_next[0]
            w_next[0] += 1
            while w_issued[0] <= min(k + 1, len(wsched) - 1):
                issue_w(w_issued[0])
                w_issued[0] += 1
            return k % 2

        small_loads = [
            (gcol, gcol_d), (cwcol, cw_d), (cbcol, cb_d), (mlncol, mln_d), (mskcol, msk_d),
            (hncol, hn_d), (lbp, lbp_d), (bg, bg_d), (m0T, m0T_d), (nst, snT_d),
        ]
        P.dma("sp", [lambda e, a=a, b=b: e.dma_start(out=a[:], in_=b) for a, b in small_loads], "consts",
              writes=["consts"])

        G(lambda e: e.memset(identf[:], 1.0), w=["identf"])
        G(lambda e: e.affine_select(out=identf[:], in_=identf[:], pattern=[[-1, 128]], compare_op=ALU.is_equal,
                                    fill=0.0, base=0, channel_multiplier=1), r=["identf"], w=["identf"])
        G(lambda e: e.memset(onesf[:], 1.0), w=["onesf"])
        V(lambda e: e.tensor_copy(out=identb[:], in_=identf[:]), r=["identf"], w=["identb"])
        G(lambda e: e.affine_select(out=cmask[:], in_=onesf[:], pattern=[[1, 128]], compare_op=ALU.is_ge,
                                    fill=0.0, base=0, channel_multiplier=-1), r=["onesf"], w=["cmask"])
        V(lambda e: e.tensor_copy(out=mask2[:], in_=cmask[:]), r=["cmask"], w=["mask2"])
        V(lambda e: e.memset(mask2[0:64, 64:128], 0.0), r=["mask2"], w=["mask2"])
        G(lambda e: e.affine_select(out=bm16f[:], in_=onesf[:, 0:16], pattern=[[-8, 16]], compare_op=ALU.is_ge,
                                    fill=0.0, base=0, channel_multiplier=1), r=["onesf"], w=["bm16f"])
        G(lambda e: e.affine_select(out=bm16f[:], in_=bm16f[:], pattern=[[8, 16]], compare_op=ALU.is_ge,
                                    fill=0.0, base=7, channel_multiplier=-1), r=["bm16f"], w=["bm16f"])
        V(lambda e: e.tensor_copy(out=bm16[:], in_=bm16f[:]), r=["bm16f"], w=["bm16"])
        V(lambda e: e.tensor_tensor(out=bmask[:].rearrange("p (b j) -> p b j", j=8),
                                    in0=cmask[:].rearrange("p (b j) -> p b j", j=8),
                                    in1=bm16[:, :].unsqueeze(2).to_broadcast([128, 16, 8]), op=ALU.mult),
          r=["cmask", "bm16"], w=["bmask"])
        G(lambda e: e.memset(msk64[:], 1.0), w=["msk64"])
        G(lambda e: e.memset(msk64[:, 0:1024].rearrange("p (c t) -> p c t", t=64)[:, :, 0:1], 0.0), r=["msk64"], w=["msk64"])
        G(lambda e: e.memset(msk64[:, 1024:1152].rearrange("p (c t) -> p c t", t=8)[:, :, 0:1], 0.0), r=["msk64"], w=["msk64"])
        G(lambda e: e.memset(one4[:], 1.0), w=["one4"])
        G(lambda e: e.memset(epsc[:], EPS), w=["epsc"])
        G(lambda e: e.memset(ginit[:], 0.0), w=["ginit"])
        G(lambda e: e.memset(minit[:], 0.0), w=["minit"])
        G(lambda e: e.memset(CF[:], 0.0), w=["CF0", "CF1", "CF2", "CF3"])
        G(lambda e: e.memset(SF[:], 0.0), w=["SF%d" % h for h in range(8)])
        V(lambda e: e.tensor_tensor(out=lbc[:], in0=lbp[:, 0, :], in1=lbp[:, 1, :], op=ALU.subtract), r=["consts"], w=["lbc"])
        A(lambda e: e.activation(out=lbc[:], in_=lbc[:], func=AF.Sigmoid), r=["lbc"], w=["lbc"])
        V(lambda e: e.tensor_scalar(out=omlc[:], in0=lbc[:], scalar1=-1.0, scalar2=1.0, op0=ALU.mult, op1=ALU.add), r=["lbc"], w=["omlc"])
        V(lambda e: e.tensor_scalar(out=nomlc[:], in0=omlc[:], scalar1=-1.0, scalar2=None, op0=ALU.mult), r=["omlc"], w=["nomlc"])
        V(lambda e: e.tensor_scalar(out=nbgf[:], in0=bg[:, 1:2], scalar1=-1.0, scalar2=None, op0=ALU.mult), r=["consts"], w=["nbgf"])
        V(lambda e: e.tensor_scalar(out=homlc[:], in0=omlc[:], scalar1=0.5, scalar2=None, op0=ALU.mult), r=["omlc"], w=["lbk"])
        V(lambda e: e.tensor_scalar(out=nhomlc[:], in0=omlc[:], scalar1=-0.5, scalar2=None, op0=ALU.mult), r=["omlc", "lbk"], w=["lbk"])
        V(lambda e: e.tensor_tensor(out=lbhc[:], in0=lbc[:], in1=homlc[:], op=ALU.add), r=["lbc", "lbk"], w=["lbk"])
        V(lambda e: e.tensor_scalar(out=mlnh[:], in0=mlncol[:], scalar1=0.5, scalar2=None, op0=ALU.mult), r=["consts"], w=["mlnh"])

        with ExitStack() as s0:
            wqT = sbt(s0, "wqT", [128, 4, 256], F32)
            wkT = sbt(s0, "wkT", [128, 4, 256], F32)
            wvT = sbt(s0, "wvT", [128, 4, 2, 256], F32)
            wg = sbt(s0, "wg", [128, 16, 8], F32)
            P.dma("sp", [lambda e: e.dma_start(out=wqT[:], in_=wqT_d), lambda e: e.dma_start(out=wkT[:], in_=wkT_d),
                         lambda e: e.dma_start(out=wvT[:], in_=wvT_d), lambda e: e.dma_start(out=wg[:], in_=wg_d)],
                  "foldw", writes=["foldw"])
            b = nb()
            for h in range(4):
                for dc in range(2):
                    c = 2 * h + dc
                    MM(banks[b][:, c * 8:(c + 1) * 8], wqT[:, h, dc * 128:(dc + 1) * 128], wg[:, h, :], True, False,
                       ["foldw"], [bk(b)], False)
                    MM(banks[b][:, c * 8:(c + 1) * 8], wkT[:, h, dc * 128:(dc + 1) * 128], wg[:, 4 + h, :], False, True,
                       ["foldw"], [bk(b)], False)
                    c2 = 8 + c
                    for vc in range(2):
                        MM(banks[b][:, c2 * 8:(c2 + 1) * 8], wvT[:, h, vc, dc * 128:(dc + 1) * 128], wg[:, 8 + 2 * h + vc, :],
                           vc == 0, vc == 1, ["foldw"], [bk(b)], (c == 7 and vc == 1))
            pv = banks[b][:, 0:128].rearrange("p (c g) -> p c g", g=8)
            V(lambda e: e.tensor_copy(out=GI[:], in_=pv[:, :, 0:4]), r=[bk(b)], w=["GI"])
            V(lambda e: e.tensor_copy(out=GF[:], in_=pv[:, :, 4:8]), r=[bk(b)], w=["GF"])
            P.full_barrier()

        issue_w(0)
        w_issued[0] = 1
        ck("consts")
        for ps_ in range(2):
            NTp = 8 if ps_ == 0 else 9
            W = NTp * 128
            has_s = (ps_ == 1)
            tgs = [(0, 512), (512, 512)] + ([(1024, 128)] if has_s else [])
            gt = lambda i: (ps_ * 8 + i) if i < 8 else 16

            with ExitStack() as s1:
                xt = [sbt(s1, "xt%d" % i, [128, 1024], F32) for i in range(3)]
                xsb = [sbt(s1, "xsb%d" % i, [128, 1024], BF16) for i in range(2)]
                junk = sbt(s1, "junk", [128, 1024], BF16)
                ss1 = sbt(s1, "ss1", [128, 9], F32)
                rs1 = sbt(s1, "rs1", [128, 9], F32)
                pend_ev = [None]
                for i in range(NTp):
                    sl = i % 3
                    s2 = i % 2
                    P.dma("sp", lambda e, i=i, sl=sl: e.dma_start(out=xt[sl][:], in_=x_d[gt(i)]), "xt%d" % sl,
                          writes=["xt%d" % sl])
                    A(lambda e, i=i, sl=sl: e.activation(out=junk[:], in_=xt[sl][:], func=AF.Square,
                                                          accum_out=ss1[:, i:i + 1]),
                      r=["xt%d" % sl], w=["junk", "ss1_%d" % i])
                    A(lambda e, i=i: e.activation(out=rs1[:, i:i + 1], in_=ss1[:, i:i + 1], func=AF.Ln,
                                                  scale=1.0 / 1024.0, bias=epsc[:, 0:1]),
                      r=["ss1_%d" % i, "epsc"], w=["rs1_%d" % i])
                    A(lambda e, i=i: e.activation(out=rs1[:, i:i + 1], in_=rs1[:, i:i + 1], func=AF.Exp, scale=-0.5),
                      r=["rs1_%d" % i], w=["rs1_%d" % i])
                    V(lambda e, i=i, sl=sl, s2=s2: e.tensor_scalar(out=xsb[s2][:], in0=xt[sl][:], scalar1=rs1[:, i:i + 1],
                                                                   scalar2=None, op0=ALU.mult),
                      r=["xt%d" % sl, "rs1_%d" % i], w=["xsb%d" % s2])
                    b = nb()
                    for k in range(8):
                        TR(bbf(b)[:, k * 128:(k + 1) * 128], xsb[s2][:, k * 128:(k + 1) * 128], identb[:],
                           ["xsb%d" % s2, "identb"], [bk(b)], k == 7)
                    def evac_x(i=i, b=b):
                        V(lambda e: e.tensor_tensor(out=xnT[:, :, i * 128:(i + 1) * 128],
                                                    in0=bbf(b).rearrange("p (k t) -> p k t", k=8),
                                                    in1=gcol[:, :].unsqueeze(2).to_broadcast([128, 8, 128]), op=ALU.mult),
                          r=[bk(b), "consts"], w=["xnT_%d" % i])
                    if pend_ev[0] is not None:
                        pend_ev[0]()
                    pend_ev[0] = evac_x
                pend_ev[0]()
                pend_ev[0] = None
                P.full_barrier()
            ck("p%d_ph1" % ps_)
            xn_all = ["xnT_%d" % i for i in range(NTp)]

            def xn_keys(c0, n):
                return ["xnT_%d" % i for i in range(c0 // 128, (c0 + n) // 128)]

            def inproj_fm(slot, dcol, evac):
                for (c0, n) in tgs:
                    b = nb()
                    for k in range(8):
                        MM(banks[b][:, 0:n], wst[slot][:, k, dcol:dcol + 128], xnT[:, k, c0:c0 + n], k == 0, k == 7,
                           ["wst%d" % slot] + xn_keys(c0, n), [bk(b)], k == 7)
                    evac(b, c0, n)

            with ExitStack() as s2_:
                wq_bf = sbt(s2_, "wq_bf", [128, 4, 2, 128], BF16)
                wk_bf = sbt(s2_, "wk_bf", [128, 4, 2, 128], BF16)
                wv_bf = sbt(s2_, "wv_bf", [128, 4, 2, 256], BF16)
                P.dma("pool", [
                    lambda e: e.dma_start(out=wq_bf[:], in_=wq_d.rearrange("h (c p) k -> p h c k", p=128)),
                    lambda e: e.dma_start(out=wk_bf[:], in_=wk_d.rearrange("h (c p) k -> p h c k", p=128)),
                    lambda e: e.dma_start(out=wv_bf[:], in_=wv_d.rearrange("h (c p) k -> p h c k", p=128)),
                ], "wsmall", writes=["wqkv"])
                xmT = sbt(s2_, "xmT", [128, 8, 1028], BF16)
                xmS = sbt(s2_, "xmS", [128, 8, 11, 16], BF16)
                xmSc = sbt(s2_, "xmSc", [128, 8, 128], BF16)
                xcT = sbt(s2_, "xcT", [128, 8, 1152], BF16)
                dg = [sbt(s2_, "dg%d" % i, [128, 4, 128], BF16) for i in range(2)]
                xmtok = [sbt(s2_, "xmtok%d" % i, [128, 1024], F32) for i in range(2)] if has_s else None

                if ps_ == 0:
                    V(lambda e: e.memset(xmT[:, :, 0:4], 0.0), w=["xmTpad"])
                else:
                    V(lambda e: e.tensor_copy(out=xmT[:, :, 1:4], in_=xmtail[:]), r=["xmtail"], w=["xmTpad"])
                    with ExitStack() as s2a:
                        sct = sbt(s2a, "sct", [48, 1024], F32)
                        P.dma("sp", lambda e: e.dma_start(out=sct[:], in_=sconv_d), "sct", writes=["sct"])
                        b = nb()
                        for k in range(8):
                            TR(banks[b][:, k * 48:(k + 1) * 48], sct[0:48, k * 128:(k + 1) * 128], identf[0:48, 0:48],
                               ["sct", "identf"], [bk(b)], k == 7)
                        V(lambda e, b=b: e.tensor_copy(out=xmS[:, :, 0:3, :],
                                                       in_=banks[b][:, 0:384].rearrange("p (k b j) -> p k j b", k=8, b=16)),
                          r=[bk(b)], w=["xmSpad"])
                        P.full_barrier()

                def xm_keys(k):
                    return ["xmT_%d" % k, "xmTpad", "xmSpad"]

                def conv_chunk(kc):
                    d = kc % 2
                    V(lambda e, d=d, kc=kc: e.tensor_tensor(
                        out=dg[d][:], in0=identb[:, :].unsqueeze(1).to_broadcast([128, 4, 128]),
                        in1=cwcol[:, kc, :].unsqueeze(2).to_broadcast([128, 4, 128]), op=ALU.mult),
                      r=["identb", "consts"], w=["dg%d" % d])
                    for (c0, n) in tgs:
                        b = nb()
                        for j in range(4):
                            rhs = (xmT[:, kc, c0 + j + 1:c0 + j + 1 + n] if c0 < 1024 else
                                   xmS[:, kc, :, :].rearrange("p t b -> p (t b)")[:, j * 16:j * 16 + 128])
                            MM(banks[b][:, 0:n], dg[d][:, j, :], rhs, j == 0, j == 3,
                               ["dg%d" % d] + xm_keys(kc), [bk(b)], j == 3)
                        if c0 < 1024:
                            A(lambda e, b=b, c0=c0, n=n, kc=kc: e.activation(out=xcT[:, kc, c0:c0 + n], in_=banks[b][:, 0:n],
                                                                              func=AF.Silu, bias=cbcol[:, kc:kc + 1]),
                              r=[bk(b), "consts"], w=["xcT_%d" % kc])
                        else:
                            A(lambda e, b=b, kc=kc: e.activation(out=xcT[:, kc, 1024:1152].rearrange("p (b t) -> p b t", t=8),
                                                                  in_=banks[b][:, 0:128].rearrange("p (t b) -> p b t", b=16),
                                                                  func=AF.Silu, bias=cbcol[:, kc:kc + 1]),
                              r=[bk(b), "consts"], w=["xcT_%d" % kc])

                for blk in range(2):
                    slot = acquire_w()
                    for sub in range(4):
                        kc = blk * 4 + sub

                        def ev_xm(b, c0, n, kc=kc):
                            if c0 < 1024:
                                A(lambda e: e.activation(func=AF.Copy, out=xmT[:, kc, 4 + c0:4 + c0 + n], in_=banks[b][:, 0:n]),
                                  r=[bk(b)], w=["xmT_%d" % kc])
                            else:
                                A(lambda e: e.activation(func=AF.Copy, out=xmS[:, kc, 3:11, :],
                                                   in_=banks[b][:, 0:128].rearrange("p (b j) -> p j b", j=8)),
                                  r=[bk(b)], w=["xmT_%d" % kc])
                                A(lambda e: e.activation(func=AF.Copy, out=xmSc[:, kc, :], in_=banks[b][:, 0:128]), r=[bk(b)], w=["xmSc_%d" % kc])
                        inproj_fm(slot, sub * 128, ev_xm)
                        if kc > 0:
                            conv_chunk(kc - 1)
                    if has_s:
                        for ti, c0 in ((0, 896), (1, 1024)):
                            b = nb()
                            for k in range(8):
                                MM(banks[b][:, :], xnT[:, k, c0:c0 + 128], wst[slot][:, k, :], k == 0, k == 7,
                                   ["wst%d" % slot] + xn_keys(c0, 128), [bk(b)], k == 7)
                            A(lambda e, b=b, ti=ti, blk=blk: e.activation(func=AF.Copy, out=xmtok[ti][:, blk * 512:(blk + 1) * 512], in_=banks[b][:, :]),
                              r=[bk(b)], w=["xmtok%d_%d" % (ti, blk)])
                conv_chunk(7)
                if has_s:
                    P.dma("sp", lambda e: e.dma_start(out=pconv_d, in_=xmtok[0][125:128, :]), "pconv",
                          reads=["xmtok0_0", "xmtok0_1"])
                    P.dma("sp", [lambda e, b_=b_: e.dma_start(out=sconvo_d[b_], in_=xmtok[1][8 * b_ + 5:8 * b_ + 8, :])
                                 for b_ in range(16)], "sconvo", reads=["xmtok1_0", "xmtok1_1"])

                ck("p%d_ph2a" % ps_)
                xc_all = ["xcT_%d" % k for k in range(8)]
                xm_all = ["xmT_%d" % k for k in range(8)] + ["xmTpad", "xmSpad"]

                with ExitStack() as sg:
                    mskg = sbt(sg, "mskg", [4, 1152], F32)
                    bigm = sbt(sg, "bigm", [4, 128], F32)
                    G(lambda e: e.memset(mskg[:], 1.0), w=["mskg"])
                    G(lambda e: e.memset(mskg[:, 1024:1152].rearrange("p (c t) -> p c t", t=8)[:, :, 0:1], 0.0), r=["mskg"], w=["mskg"])
                    G(lambda e: e.memset(bigm[:], 1e30), w=["bigm"])
                    G(lambda e: e.memset(bigm[:, :].rearrange("p (c t) -> p c t", t=8)[:, :, 0:1], -1e30), r=["bigm"], w=["bigm"])
                    IG = sbt(sg, "IG", [4, 1152], F32)
                    SPt = sbt(sg, "SPt", [4, 1152], F32)
                    GN = sbt(sg, "GN", [4, 1152], F32)
                    AG = sbt(sg, "AG", [4, 1152], F32)
                    MT = sbt(sg, "MT", [4, 1152], F32)
                    T1 = sbt(sg, "T1", [4, 1152], F32)
                    T3 = sbt(sg, "T3", [4, 1152], F32)
                    Rall = sbt(sg, "Rall", [4, 24], F32)
                    Rprev = sbt(sg, "Rprev", [4, 24], F32)
                    DARG = sbt(sg, "DARG", [4, 24], F32)
                    Dblk = sbt(sg, "Dblk", [4, 4, 24], F32)
                    mo = sbt(sg, "mo", [4, 17], F32)
                    for (c0, n) in tgs:
                        bi, bf_ = nb(), nb()
                        for (bb, Gw) in ((bi, GI), (bf_, GF)):
                            for c in range(16):
                                kc = c % 8
                                if c < 8:
                                    rhs = xcT[:, kc, c0:c0 + n]
                                    rk = ["xcT_%d" % kc]
                                else:
                                    rhs = xmT[:, kc, 4 + c0:4 + c0 + n] if c0 < 1024 else xmSc[:, kc, :]
                                    rk = ["xmT_%d" % kc, "xmSc_%d" % kc]
                                MM(banks[bb][0:4, 0:n], Gw[:, c, :], rhs, c == 0, c == 15, rk + ["GI", "GF"], [bk(bb)], c == 15)
                        A(lambda e, bi=bi, c0=c0, n=n: e.activation(out=IG[:, c0:c0 + n], in_=banks[bi][0:4, 0:n], func=AF.Identity,
                                                                     bias=bg[:, 0:1]), r=[bk(bi), "consts"], w=["IG"])
                        A(lambda e, bf_=bf_, c0=c0, n=n: e.activation(out=SPt[:, c0:c0 + n], in_=banks[bf_][0:4, 0:n], func=AF.Exp,
                                                                       scale=-1.0, bias=nbgf[:, 0:1]), r=[bk(bf_), "nbgf"], w=["SPt"])
                    A(lambda e: e.activation(out=SPt[:, 0:W], in_=SPt[:, 0:W], func=AF.Ln, bias=one4[0:4, 0:1]), r=["SPt", "one4"], w=["SPt"])
                    V(lambda e: e.tensor_tensor_scan(out=GN[:, 0:W], data0=mskg[:, 0:W], data1=SPt[:, 0:W], initial=ginit[:, 0:1],
                                                     op0=ALU.mult, op1=ALU.add), r=["mskg", "SPt", "ginit"], w=["GN"])
                    V(lambda e: e.tensor_tensor(out=AG[:, 0:W], in0=IG[:, 0:W], in1=GN[:, 0:W], op=ALU.add), r=["IG", "GN"], w=["AG"])
                    V(lambda e: e.tensor_tensor_scan(out=MT[:, 0:1024], data0=AG[:, 0:1024], data1=AG[:, 0:1024], initial=minit[:, 0:1],
                                                     op0=ALU.max, op1=ALU.max), r=["AG", "minit"], w=["MT"])
                    if has_s:
                        agv = AG[:, 1024:1152].rearrange("p (b j) -> p b j", j=8)
                        V(lambda e: e.tensor_tensor(out=agv[:, :, 0:1], in0=agv[:, :, 0:1], in1=m0T[:, :].unsqueeze(2), op=ALU.max),
                          r=["AG", "consts"], w=["AGs"])
                        V(lambda e: e.tensor_tensor_scan(out=MT[:, 1024:1152], data0=bigm[:, :], data1=AG[:, 1024:1152], initial=0.0,
                                                         op0=ALU.min, op1=ALU.max), r=["AGs", "AG", "bigm"], w=["MTs"])
                        V(lambda e: e.tensor_tensor(out=AG[:, 1024:1152], in0=IG[:, 1024:1152], in1=GN[:, 1024:1152], op=ALU.add),
                          r=["IG", "GN", "MTs"], w=["AG"])
                    mtk = ["MT", "MTs"]
                    V(lambda e: e.tensor_copy(out=Rall[:, 0:8].unsqueeze(2), in_=MT[:, 0:1024].rearrange("p (c t) -> p c t", t=128)[:, :, 127:128]),
                      r=mtk, w=["Rall"])
                    V(lambda e: e.tensor_copy(out=Rprev[:, 0:1], in_=minit[:, 0:1]), r=["minit"], w=["Rprev"])
                    V(lambda e: e.tensor_copy(out=Rprev[:, 1:8], in_=Rall[:, 0:7]), r=["Rall", "Rprev"], w=["Rprev"])
                    if has_s:
                        V(lambda e: e.tensor_copy(out=Rall[:, 8:24].unsqueeze(2), in_=MT[:, 1024:1152].rearrange("p (b j) -> p b j", j=8)[:, :, 7:8]),
                          r=mtk + ["Rall"], w=["Rall"])
                        V(lambda e: e.tensor_copy(out=Rprev[:, 8:24], in_=m0T[:, :]), r=["consts", "Rprev"], w=["Rprev"])
                    else:
                        V(lambda e: e.memset(Rall[:, 8:24], 0.0), r=["Rall"], w=["Rall"])
                        V(lambda e: e.memset(Rprev[:, 8:24], 0.0), r=["Rprev"], w=["Rprev"])
                    for (src, dst, nm) in ((AG, T1, "T1"), (GN, T3, "T3")):
                        V(lambda e, src=src, dst=dst: e.tensor_tensor(
                            out=dst[:, 0:1024].rearrange("p (c t) -> p c t", t=128), in0=src[:, 0:1024].rearrange("p (c t) -> p c t", t=128),
                            in1=Rall[:, 0:8].unsqueeze(2).to_broadcast([4, 8, 128]), op=ALU.subtract), r=["AG", "GN", "Rall"], w=[nm])
                        if has_s:
                            V(lambda e, src=src, dst=dst: e.tensor_tensor(
                                out=dst[:, 1024:1152].rearrange("p (c t) -> p c t", t=8), in0=src[:, 1024:1152].rearrange("p (c t) -> p c t", t=8),
                                in1=Rall[:, 8:24].unsqueeze(2).to_broadcast([4, 16, 8]), op=ALU.subtract), r=["AG", "GN", "Rall", nm], w=[nm])
                        A(lambda e, dst=dst: e.activation(out=dst[:, 0:W], in_=dst[:, 0:W], func=AF.Exp), r=[nm], w=[nm])
                    V(lambda e: e.tensor_tensor(out=DARG[:], in0=Rprev[:], in1=Rall[:], op=ALU.subtract), r=["Rprev", "Rall"], w=["DARG"])
                    if has_s:
                        V(lambda e: e.tensor_tensor(out=mo[:, 0:1], in0=MT[:, 1023:1024], in1=GN[:, 1023:1024], op=ALU.subtract),
                          r=mtk + ["GN"], w=["mo"])
                        V(lambda e: e.tensor_tensor(out=mo[:, 1:17].unsqueeze(2), in0=MT[:, 1024:1152].rearrange("p (b j) -> p b j", j=8)[:, :, 7:8],
                                                    in1=GN[:, 1024:1152].rearrange("p (b j) -> p b j", j=8)[:, :, 7:8], op=ALU.subtract),
                          r=mtk + ["GN", "mo"], w=["mo"])
                        P.dma("sp", [lambda e: e.dma_start(out=pmT_d, in_=mo[:, 0:1]),
                                     lambda e: e.dma_start(out=smTo_d, in_=mo[:, 1:17])], "mo", reads=["mo"])
                    V(lambda e: e.tensor_copy(out=ginit[:, 0:1], in_=GN[:, 1023:1024]), r=["GN", "Rprev"], w=["ginit"])
                    V(lambda e: e.tensor_copy(out=minit[:, 0:1], in_=MT[:, 1023:1024]), r=mtk + ["Rprev"], w=["minit"])
                    b = nb()
                    for i in range(NTp):
                        for qi, Q in enumerate((T1, T3)):
                            o = (i * 2 + qi) * 4
                            TR(banks[b][:, o:o + 4], Q[0:4, i * 128:(i + 1) * 128], identf[0:4, 0:4], ["T1", "T3", "identf"], [bk(b)],
                               (i == NTp - 1 and qi == 1))
                    V(lambda e, b=b: e.tensor_copy(out=tokS[:, 0:NTp].rearrange("p i q h -> p (i q h)"), in_=banks[b][:, 0:NTp * 8]),
                      r=[bk(b)], w=["tokS"])
                    V(lambda e: e.tensor_tensor(out=Dblk[:], in0=DARG[:, :].unsqueeze(1).to_broadcast([4, 4, 24]),
                                                in1=identf[0:4, 0:4].unsqueeze(2).to_broadcast([4, 4, 24]), op=ALU.mult),
                      r=["DARG", "identf"], w=["Dblk"])
                    b = nb()
                    MM(banks[b][:, 0:96], onesf[0:4, :], Dblk[:].rearrange("p h c -> p (h c)"), True, True, ["onesf", "Dblk"], [bk(b)], True)
                    A(lambda e, b=b: e.activation(out=DECb[:].rearrange("p h c -> p (h c)"), in_=banks[b][:, 0:96], func=AF.Exp),
                      r=[bk(b)], w=["DECb"])
                    P.full_barrier()

                ck("p%d_gates" % ps_)
                with ExitStack() as sh:
                    HS = []
                    NHS = 4
                    for p_ in range(NHS):
                        HS.append(dict(
                            qT=sbt(sh, "qT%d" % p_, [128, 1152], BF16), kT=sbt(sh, "kT%d" % p_, [128, 1152], BF16),
                            ktil=sbt(sh, "ktil%d" % p_, [128, 9, 128], BF16), v1=sbt(sh, "v1%d" % p_, [128, 9, 258], BF16),
                            sTsb=[sbt(sh, "sTsb%d_%d" % (p_, i), [128, 128], BF16) for i in range(2)],
                            Cd=sbt(sh, "Cd%d" % p_, [128, 258], BF16),
                            hnsb=[sbt(sh, "hnsb%d_%d" % (p_, i), [128, 256], BF16) for i in range(2)],
                            st6=sbt(sh, "st6%d" % p_, [128, 2, 6], F32), mv=sbt(sh, "mv%d" % p_, [128, 2, 2], F32),
                            wv=sbt(sh, "wv%d" % p_, [128, 2, 4], F32)))
                        G(lambda e, p_=p_: e.memset(HS[p_]["v1"][:, :, 256:257], 1.0), w=["v1ones%d" % p_])
                    if has_s:
                        Csts = [sbt(sh, "Cst%d" % i, [128, 4, 257], F32) for i in range(4)]

                        def issue_cst(r_):
                            h_, g_ = r_ // 4, r_ % 4
                            cb = Csts[r_ % 4]
                            P.dma("sp", lambda e: e.dma_start(
                                out=cb[:, :, 0:256], in_=sC_d[g_ * 4:(g_ + 1) * 4, h_].rearrange("b k v -> k b v")),
                                "Cst%d" % (r_ % 4), writes=["Cst%d" % (r_ % 4)])
                        Cdb = sbt(sh, "Cdb", [128, 4, 258], BF16)
                        Qpad = [sbt(sh, "Qpad%d" % i, [128, 640], BF16) for i in range(4)]
                        Kpad = sbt(sh, "Kpad", [128, 4, 128], BF16)
                        for i_q in range(4):
                            G(lambda e: e.memset(Qpad[i_q][:], 0.0), w=["Qpad%d" % i_q])

                    def out_stage(h, i, bnum, par):
                        B = HS[h % NHS]
                        st6, mv, wv_, hnsb = B["st6"], B["mv"], B["wv"], B["hnsb"]
                        pn_ = banks[bnum]
                        kq = "ost%d_%d" % (h % NHS, par)
                        hk = "hnsb%d_%d" % (h % NHS, par)
                        V(lambda e: e.bn_stats(out=st6[:, par, :], in_=pn_[:, 0:256]), r=[bk(bnum)], w=[kq + "a"])
                        V(lambda e: e.bn_aggr(out=mv[:, par, :], in_=st6[:, par, :]), r=[kq + "a"], w=[kq + "b"])
                        V(lambda e: e.tensor_scalar(out=wv_[:, par, 1:2], in0=pn_[:, 256:257], scalar1=-1.0, scalar2=None, op0=ALU.mult),
                          r=[bk(bnum)], w=[kq + "c0"])
                        V(lambda e: e.scalar_tensor_tensor(out=wv_[:, par, 0:1], in0=pn_[:, 256:257], scalar=tokS[:, i, 1, h:h + 1],
                                                           in1=wv_[:, par, 1:2], op0=ALU.max, op1=ALU.max),
                          r=[bk(bnum), "tokS", kq + "c0"], w=[kq + "c"])
                        V(lambda e: e.tensor_tensor(out=wv_[:, par, 1:2], in0=wv_[:, par, 0:1], in1=wv_[:, par, 0:1], op=ALU.mult),
                          r=[kq + "c", kq + "c0"], w=[kq + "d", kq + "c0"])
                        V(lambda e: e.scalar_tensor_tensor(out=wv_[:, par, 2:3], in0=wv_[:, par, 1:2], scalar=EPS, in1=mv[:, par, 1:2],
                                                           op0=ALU.mult, op1=ALU.add), r=[kq + "d", kq + "b"], w=[kq + "e"])
                        A(lambda e: e.activation(out=wv_[:, par, 3:4], in_=wv_[:, par, 2:3], func=AF.Ln), r=[kq + "e"], w=[kq + "f"])
                        A(lambda e: e.activation(out=wv_[:, par, 3:4], in_=wv_[:, par, 3:4], func=AF.Exp, scale=-0.5), r=[kq + "f"], w=[kq + "f"])
                        V(lambda e: e.tensor_scalar(out=hnsb[par][:], in0=pn_[:, 0:256], scalar1=mv[:, par, 0:1], scalar2=wv_[:, par, 3:4],
                                                    op0=ALU.subtract, op1=ALU.mult), r=[bk(bnum), kq + "b", kq + "f"], w=[hk])
                        bt = nb()
                        for c in range(2):
                            TR(bbf(bt)[:, c * 128:(c + 1) * 128], hnsb[par][:, c * 128:(c + 1) * 128], identb[:], [hk, "identb"],
                               [bk(bt)], c == 1)
                        A(lambda e: e.activation(func=AF.Copy, out=mixA[:, 2 * h:2 * h + 2, i * 128:(i + 1) * 128],
                                                 in_=bbf(bt)[:, 0:256].rearrange("p (c t) -> p c t", c=2)), r=[bk(bt)], w=["mixA_%d" % h])

                    D4 = sbt(sh, "D4", [128, 2, 4], F32)
                    mv4 = sbt(sh, "mv4", [128, 2, 4, 2], F32)
                    w4 = sbt(sh, "w4", [128, 2, 4, 4], F32)
                    pend = {}

                    def out_stage4(i):
                        par = i % 2
                        kq = "os4_%d" % par
                        for h in range(4):
                            B = HS[h]
                            pn_ = banks[pend[h]]
                            V(lambda e: e.bn_stats(out=B["st6"][:, par, :], in_=pn_[:, 0:256]), r=[bk(pend[h])], w=[kq + "s%d" % h])
                            V(lambda e: e.bn_aggr(out=mv4[:, par, h, :], in_=B["st6"][:, par, :]), r=[kq + "s%d" % h], w=[kq + "mv%d" % h])
                            V(lambda e: e.tensor_copy(out=D4[:, par, h:h + 1], in_=pn_[:, 256:257]), r=[bk(pend[h])],
                              w=[kq + "d%d" % h])
                        dk = [kq + "d%d" % h for h in range(4)]
                        mk = [kq + "mv%d" % h for h in range(4)]
                        wa, wb, wc, wd = (w4[:, par, :, j] for j in range(4))
                        V(lambda e: e.tensor_scalar(out=wa, in0=D4[:, par, :], scalar1=-1.0, scalar2=None, op0=ALU.mult), r=dk, w=[kq + "a"])
                        V(lambda e: e.tensor_tensor(out=wb, in0=D4[:, par, :], in1=tokS[:, i, 1, :], op=ALU.max), r=dk + ["tokS"], w=[kq + "b"])
                        V(lambda e: e.tensor_tensor(out=wb, in0=wb, in1=wa, op=ALU.max), r=[kq + "a", kq + "b"], w=[kq + "b"])
                        V(lambda e: e.tensor_tensor(out=wc, in0=wb, in1=wb, op=ALU.mult), r=[kq + "b"], w=[kq + "c"])
                        V(lambda e: e.scalar_tensor_tensor(out=wc, in0=wc, scalar=EPS, in1=mv4[:, par, :, 1], op0=ALU.mult, op1=ALU.add),
                          r=[kq + "c"] + mk, w=[kq + "c"])
                        A(lambda e: e.activation(out=wd, in_=wc, func=AF.Ln), r=[kq + "c"], w=[kq + "e"])
                        A(lambda e: e.activation(out=wd, in_=wd, func=AF.Exp, scale=-0.5), r=[kq + "e"], w=[kq + "e"])
                        for h in range(4):
                            B = HS[h]
                            pn_ = banks[pend[h]]
                            hk = "hnsb%d_%d" % (h, par)
                            hnsb = B["hnsb"]
                            V(lambda e: e.tensor_scalar(out=hnsb[par][:], in0=pn_[:, 0:256], scalar1=mv4[:, par, h, 0:1],
                                                        scalar2=w4[:, par, h, 3:4], op0=ALU.subtract, op1=ALU.mult),
                              r=[bk(pend[h]), kq + "e"] + mk, w=[hk])
                            bt = nb()
                            for c in range(2):
                                TR(bbf(bt)[:, c * 128:(c + 1) * 128], hnsb[par][:, c * 128:(c + 1) * 128], identb[:], [hk, "identb"],
                                   [bk(bt)], c == 1)
                            A(lambda e: e.activation(func=AF.Copy, out=mixA[:, 2 * h:2 * h + 2, i * 128:(i + 1) * 128],
                                                     in_=bbf(bt)[:, 0:256].rearrange("p (c t) -> p c t", c=2)), r=[bk(bt)], w=["mixA_%d" % h])

                    def proj(h):
                        B = HS[h % NHS]
                        p_ = h % NHS
                        qT, kT, ktil, v1 = B["qT"], B["kT"], B["ktil"], B["v1"]
                        for (c0, n) in tgs:
                            bq, bk_ = nb(), nb()
                            for dc in range(2):
                                MM(banks[bq][:, 0:n], wq_bf[:, h, dc, :], xcT[:, 2 * h + dc, c0:c0 + n], dc == 0, dc == 1,
                                   ["wqkv", "xcT_%d" % (2 * h + dc)], [bk(bq)], dc == 1)
                            for dc in range(2):
                                MM(banks[bk_][:, 0:n], wk_bf[:, h, dc, :], xcT[:, 2 * h + dc, c0:c0 + n], dc == 0, dc == 1,
                                   ["wqkv", "xcT_%d" % (2 * h + dc)], [bk(bk_)], dc == 1)
                            A(lambda e: e.activation(out=qT[:, c0:c0 + n], in_=banks[bq][:, 0:n], func=AF.Copy,
                                                     scale=float(128 ** -0.5)), r=[bk(bq)], w=["qT%d" % p_])
                            V(lambda e: e.tensor_copy(out=kT[:, c0:c0 + n], in_=banks[bk_][:, 0:n]), r=[bk(bk_)], w=["kT%d" % p_])
                        for i in range(NTp):
                            b = nb()
                            cs = slice(i * 128, (i + 1) * 128)
                            for dc in range(2):
                                lhs = xmT[:, 2 * h + dc, 4 + i * 128:4 + (i + 1) * 128] if i < 8 else xmSc[:, 2 * h + dc, :]
                                MM(banks[b][:, 0:256], lhs, wv_bf[:, h, dc, :], dc == 0, dc == 1,
                                   ["wqkv", "xmT_%d" % (2 * h + dc), "xmSc_%d" % (2 * h + dc)], [bk(b)], False)
                            for dc in range(2):
                                MM(banks[b][:, 256:384], xcT[:, 2 * h + dc, cs], wk_bf[:, h, dc, :], dc == 0, dc == 1,
                                   ["wqkv", "xcT_%d" % (2 * h + dc)], [bk(b)], dc == 1)
                            V(lambda e: e.tensor_scalar(out=ktil[:, i, :], in0=banks[b][:, 256:384], scalar1=tokS[:, i, 0, h:h + 1],
                                                        scalar2=None, op0=ALU.mult), r=[bk(b), "tokS"], w=["ktil%d_%d" % (p_, i)])
                            V(lambda e: e.tensor_copy(out=v1[:, i, 0:256], in_=banks[b][:, 0:256]), r=[bk(b)],
                              w=["v1%d_%d" % (p_, i)])

                    def mloop(h):
                        B = HS[h % NHS]
                        p_ = h % NHS
                        qT, kT, ktil, v1, sTsb, Cd = B["qT"], B["kT"], B["ktil"], B["v1"], B["sTsb"], B["Cd"]
                        cfk = "CF%d" % h
                        for i in range(8):
                            gi = ps_ * 8 + i
                            cs = slice(i * 128, (i + 1) * 128)
                            par = i % 2
                            sk = "sTsb%d_%d" % (p_, par)
                            vk = ["v1%d_%d" % (p_, i), "v1ones%d" % p_]
                            ba = nb()
                            MM(banks[ba][:, 0:128], kT[:, cs], qT[:, cs], True, True, ["kT%d" % p_, "qT%d" % p_], [bk(ba)], True)
                            V(lambda e: e.scalar_tensor_tensor(out=sTsb[par][:], in0=banks[ba][:, 0:128],
                                                               scalar=tokS[:, i, 0, h:h + 1], in1=cmask[:],
                                                               op0=ALU.mult, op1=ALU.mult),
                              r=[bk(ba), "tokS", "cmask"], w=[sk])
                            if gi > 0:
                                A(lambda e: e.activation(out=Cd[:, 0:257], in_=CF[:, h, :], func=AF.Copy, scale=DECb[:, h, i:i + 1]),
                                  r=[cfk, "DECb"], w=["Cd%d" % p_])
                            bn_ = 3 + h
                            MM(banks[bn_][:, 0:257], sTsb[par][:], v1[:, i, 0:257], True, gi == 0, [sk] + vk, [bk(bn_)], gi == 0)
                            if gi > 0:
                                MM(banks[bn_][:, 0:257], qT[:, cs], Cd[:, 0:257], False, True, ["qT%d" % p_, "Cd%d" % p_], [bk(bn_)], True)
                            bc = nb()
                            MM(banks[bc][:, 0:257], ktil[:, i, :], v1[:, i, 0:257], True, True, ["ktil%d_%d" % (p_, i)] + vk, [bk(bc)], True)
                            V(lambda e: e.scalar_tensor_tensor(out=CF[:, h, :], in0=CF[:, h, :], scalar=DECb[:, h, i:i + 1],
                                                               in1=banks[bc][:, 0:257], op0=ALU.mult, op1=ALU.add),
                              r=[cfk, "DECb", bk(bc)], w=[cfk])
                            pend[h] = bn_
                            yield

                    def msample(h):
                        B = HS[h % NHS]
                        p_ = h % NHS
                        qT, kT, ktil, v1, sTsb = B["qT"], B["kT"], B["ktil"], B["v1"], B["sTsb"]
                        cfk = "CF%d" % h
                        P.dma("sp", lambda e: e.dma_start(out=pC_d[h], in_=CF[:, h, 0:256]), "pC", reads=[cfk])
                        V(lambda e: e.tensor_copy(out=pnT[:, h:h + 1], in_=CF[:, h, 256:257]), r=[cfk], w=["pnT"])
                        cs = slice(1024, 1152)
                        sk = "sTsb%d_0" % p_
                        vk = ["v1%d_8" % p_, "v1ones%d" % p_]
                        ba = nb()
                        MM(banks[ba][:, 0:128], kT[:, cs], qT[:, cs], True, True, ["kT%d" % p_, "qT%d" % p_], [bk(ba)], True)
                        V(lambda e: e.scalar_tensor_tensor(out=sTsb[0][:], in0=banks[ba][:, 0:128], scalar=tokS[:, 8, 0, h:h + 1],
                                                           in1=bmask[:], op0=ALU.mult, op1=ALU.mult),
                          r=[bk(ba), "tokS", "bmask"], w=[sk])
                        bn_ = RES
                        MM(banks[bn_][:, 0:257], sTsb[0][:], v1[:, 8, 0:257], True, False, [sk] + vk, [bk(bn_)], False)
                        for grp in range(4):
                            r_ = 4 * h + grp
                            if r_ == 0:
                                issue_cst(0)
                                issue_cst(1)
                            if r_ + 2 < 16:
                                issue_cst(r_ + 2)
                            Cst = Csts[r_ % 4]
                            ck_, cnk_ = "Cst%d" % (r_ % 4), "Cstn%d" % (r_ % 4)
                            V(lambda e: e.tensor_copy(out=Cst[:, :, 256:257], in_=nst[:, grp * 4:(grp + 1) * 4, h:h + 1]),
                              r=["consts", ck_], w=[cnk_])
                            V(lambda e: e.tensor_tensor(
                                out=Cdb[:, :, 0:257], in0=Cst[:], in1=DECb[:, h, 8 + grp * 4:12 + grp * 4].unsqueeze(2).to_broadcast([128, 4, 257]),
                                op=ALU.mult), r=[ck_, cnk_, "DECb"], w=["Cdb"])
                            V(lambda e: e.tensor_copy(
                                out=Qpad[grp][:, grp * 32:grp * 32 + 544].rearrange("p (b c) -> p b c", c=136)[:, :, 0:8],
                                in_=qT[:, 1024 + grp * 32:1056 + grp * 32].rearrange("p (b j) -> p b j", j=8)),
                              r=["qT%d" % p_, "Qpad%d" % grp], w=["Qpad%d" % grp])
                            for b_ in range(4):
                                last = (grp == 3 and b_ == 3)
                                MM(banks[bn_][:, 0:257], Qpad[grp][:, b_ * 128:(b_ + 1) * 128], Cdb[:, b_, 0:257], False, last,
                                   ["Qpad%d" % grp, "Cdb"], [bk(bn_)], last)
                            V(lambda e: e.tensor_tensor(
                                out=Kpad[:], in0=ktil[:, 8, :].unsqueeze(1).to_broadcast([128, 4, 128]),
                                in1=bm16[:, grp * 4:(grp + 1) * 4].unsqueeze(2).to_broadcast([128, 4, 128]), op=ALU.mult),
                              r=["ktil%d_8" % p_, "bm16"], w=["Kpad"])
                            for b_ in range(4):
                                bc = nb()
                                MM(banks[bc][:, 0:257], Kpad[:, b_, :], v1[:, 8, 0:257], True, True, ["Kpad"] + vk, [bk(bc)], True)
                                V(lambda e: e.scalar_tensor_tensor(
                                    out=Cst[:, b_, :], in0=Cst[:, b_, :], scalar=DECb[:, h, 8 + grp * 4 + b_:9 + grp * 4 + b_],
                                    in1=banks[bc][:, 0:257], op0=ALU.mult, op1=ALU.add),
                                  r=[ck_, cnk_, "Cdb", "DECb", bk(bc)], w=[ck_, cnk_])
                            P.dma("sp", lambda e: e.dma_start(
                                out=sCo_d[grp * 4:(grp + 1) * 4, h].rearrange("b k v -> k b v"), in_=Cst[:, :, 0:256]),
                                "CstO%d" % (r_ % 4), reads=[ck_, cnk_])
                            V(lambda e: e.tensor_copy(out=snout[:, grp * 4:(grp + 1) * 4, h:h + 1], in_=Cst[:, :, 256:257]),
                              r=[ck_, cnk_], w=["snout"])
                        out_stage(h, 8, bn_, 0)

                    for h in range(4):
                        proj(h)
                    nrot[0] = 3
                    for i_, _ in enumerate(zip(mloop(0), mloop(1), mloop(2), mloop(3))):
                        out_stage4(i_)
                    nrot[0] = 6
                    if has_s:
                        for h in range(4):
                            msample(h)
                    if has_s:
                        P.dma("sp", [lambda e: e.dma_start(out=pnT_d, in_=pnT[:]),
                                     lambda e: e.dma_start(out=snTo_d, in_=snout[:])], "nout", reads=["pnT", "snout"])
                    P.full_barrier()

                ck("p%d_heads" % ps_)
                if ps_ == 0:
                    V(lambda e: e.tensor_copy(out=xmtail[:], in_=xmT[:, :, 1025:1028]), r=xm_all, w=["xmtail"])
                with ExitStack() as sgt:
                    szt = [sbt(sgt, "szt%d" % i, [128, 512], BF16) for i in range(2)]
                    sot = [sbt(sgt, "sot%d" % i, [128, 512], BF16) for i in range(2)]
                    xst = [sbt(sgt, "xst%d" % i, [128, 512], F32) for i in range(2)]
                    t1t = [sbt(sgt, "t1t%d" % i, [128, 512], F32) for i in range(2)]
                    t2t = [sbt(sgt, "t2t%d" % i, [128, 512], F32) for i in range(2)]
                    cnt = 0
                    for fc in range(8):
                        slot = acquire_w()
                        h = fc // 2
                        for (c0, n) in tgs:
                            pr = cnt % 2
                            cnt += 1
                            bz, bo = nb(), nb()
                            for k in range(8):
                                MM(banks[bz][:, 0:n], wst[slot][:, k, 0:128], xnT[:, k, c0:c0 + n], k == 0, k == 7,
                                   ["wst%d" % slot] + xn_keys(c0, n), [bk(bz)], k == 7)
                            for k in range(8):
                                MM(banks[bo][:, 0:n], wst[slot][:, k, 128:256], xnT[:, k, c0:c0 + n], k == 0, k == 7,
                                   ["wst%d" % slot] + xn_keys(c0, n), [bk(bo)], k == 7)
                            A(lambda e: e.activation(out=szt[pr][:, 0:n], in_=banks[bz][:, 0:n], func=AF.Silu), r=[bk(bz)], w=["szt%d" % pr])
                            A(lambda e: e.activation(out=sot[pr][:, 0:n], in_=banks[bo][:, 0:n], func=AF.Tanh, scale=0.5), r=[bk(bo)],
                              w=["sot%d" % pr])
                            A(lambda e: e.activation(out=xst[pr][:, 0:n], in_=xcT[:, fc, c0:c0 + n], func=AF.Copy,
                                                     scale=mskcol[:, fc:fc + 1]), r=["xcT_%d" % fc, "consts"], w=["xst%d" % pr])
                            V(lambda e: e.scalar_tensor_tensor(
                                out=t1t[pr][:, 0:n], in0=sot[pr][:, 0:n], scalar=1.0, in1=mixA[:, fc, c0:c0 + n],
                                op0=ALU.add, op1=ALU.mult), r=["mixA_%d" % h, "sot%d" % pr], w=["t1t%d" % pr])
                            V(lambda e: e.scalar_tensor_tensor(
                                out=t2t[pr][:, 0:n], in0=t1t[pr][:, 0:n], scalar=mlnh[:, fc:fc + 1], in1=xst[pr][:, 0:n],
                                op0=ALU.mult, op1=ALU.add), r=["xst%d" % pr, "t1t%d" % pr, "mlnh"], w=["t2t%d" % pr])
                            G(lambda e: e.tensor_tensor(
                                out=mixA[:, fc, c0:c0 + n], in0=t2t[pr][:, 0:n], in1=szt[pr][:, 0:n], op=ALU.mult),
                              r=["t2t%d" % pr, "szt%d" % pr, "mixA_%d" % h], w=["mixA_%d" % h])
                    P.full_barrier()

            ck("p%d_ph2" % ps_)
            s34 = ExitStack()
            mixB = sbt(s34, "mixB", [128, 8, 1152], BF16)
            with ExitStack() as s3:
                vtok = sbt(s3, "vtok", [128, 9, 1024], BF16)
                Abig = sbt(s3, "Abig", [128, 4, 1152], F32)
                HB = []
                for p_ in range(2):
                    HB.append(dict(
                        QS=None, ZS=sbt(s3, "ZS%d" % p_, [128, 1152], BF16),
                        QT=sbt(s3, "QT%d" % p_, [128, 1152], BF16), KHAT=sbt(s3, "KHAT%d" % p_, [128, 1152], BF16),
                        KTT=None,
                        KA=sbt(s3, "KA%d" % p_, [128, 8, 128], BF16), KB=sbt(s3, "KB%d" % p_, [128, 8, 128], BF16),
                        DECH=sbt(s3, "DECH%d" % p_, [128, 32], F32),
                        Sbf=[sbt(s3, "Sbf%d_%d" % (p_, i), [128, 128], BF16) for i in range(2)],
                        KS=(sbt(s3, "KS%d" % p_, [128, 128], BF16) if has_s else None)))
                    G(lambda e: e.memset(HB[p_]["KA"][:], 0.0), w=["KA%d" % p_])
                    G(lambda e: e.memset(HB[p_]["KB"][:], 0.0), w=["KB%d" % p_])
                QS1 = sbt(s3, "QS", [128, 1152], BF16)
                KTT1 = sbt(s3, "KTT", [128, 1152], BF16)
                ATsb = [sbt(s3, "ATsb%d" % i, [128, 128], BF16) for i in range(2)]
                Ohs = [sbt(s3, "Oh%d" % i, [64, 16, 128], F32) for i in range(2)]
                SQ = sbt(s3, "SQ", [64, 2048], BF16)
                SQ2 = sbt(s3, "SQ2", [64, 2048], BF16)
                On = SQ2[:, :].rearrange("p (c v) -> p c v", v=128)
                ssh = sbt(s3, "ssh", [128, 20], F32)
                ssq = [sbt(s3, "ssq%d" % i, [64, 16], F32) for i in range(2)]
                junkO = sbt(s3, "junkO", [64, 128], BF16)
                if has_s:
                    Ssts = [sbt(s3, "Sst%d" % i, [128, 16, 128], F32) for i in range(2)]

                    def issue_sst(h_):
                        P.dma("sp", lambda e: e.dma_start(out=Ssts[h_ % 2][:], in_=sS_d[:, h_].rearrange("b k v -> k b v")),
                              "Sst%d" % (h_ % 2), writes=["Sst%d" % (h_ % 2)])
                    issue_sst(0)
                    Sb16 = sbt(s3, "Sb16", [128, 16, 128], BF16)
                    QpH = sbt(s3, "QpH", [128, 2176], BF16)
                    Kp16 = sbt(s3, "Kp16", [128, 16, 128], BF16)
                    tmpS = sbt(s3, "tmpS", [128, 4, 128], F32)
                    Ons = sbt(s3, "Ons", [128, 128], BF16)
                    G(lambda e: e.memset(QpH[:], 0.0), w=["QpH"])
                A1, A2, A3, A4 = Abig[:, 0, :], Abig[:, 1, :], Abig[:, 2, :], Abig[:, 3, :]
                for blk in range(2):
                    slot = acquire_w()
                    for i in range(NTp):
                        b = nb()
                        for k in range(8):
                            MM(banks[b][:, :], xnT[:, k, i * 128:(i + 1) * 128], wst[slot][:, k, :], k == 0, k == 7,
                               ["wst%d" % slot, "xnT_%d" % i], [bk(b)], k == 7)
                        A(lambda e: e.activation(func=AF.Copy, out=vtok[:, i, blk * 512:(blk + 1) * 512], in_=banks[b][:, :]),
                          r=[bk(b)], w=["vtok_%d" % i])

                def chain(h):
                    B = HB[h % 2]
                    p_ = h % 2
                    QS, ZS, QT, KHAT, KTT, KA, KB, DECH, KS = (B[k_] for k_ in ("QS", "ZS", "QT", "KHAT", "KTT", "KA", "KB", "DECH", "KS"))
                    QS, KTT = QS1, KTT1
                    sfx = "%d" % p_
                    slot = acquire_w()
                    for (c0, n) in tgs:
                        bs = [nb(), nb(), nb()]
                        for j in range(3):
                            for k in range(8):
                                MM(banks[bs[j]][:, 0:n], wst[slot][:, k, j * 128:(j + 1) * 128], xnT[:, k, c0:c0 + n], k == 0, k == 7,
                                   ["wst%d" % slot] + xn_keys(c0, n), [bk(bs[j])], k == 7)
                        A(lambda e: e.activation(out=A1[:, c0:c0 + n], in_=banks[bs[0]][:, 0:n], func=AF.Tanh, scale=0.5),
                          r=[bk(bs[0])], w=["A1"])
                        A(lambda e: e.activation(out=QS[:, c0:c0 + n], in_=banks[bs[1]][:, 0:n], func=AF.Silu),
                          r=[bk(bs[1])], w=["QS"])
                        A(lambda e: e.activation(out=ZS[:, c0:c0 + n], in_=banks[bs[2]][:, 0:n], func=AF.Silu),
                          r=[bk(bs[2])], w=["ZS" + sfx])
                        yield
                    A(lambda e: e.activation(out=A2[:, 0:W], in_=A1[:, 0:W], func=AF.Identity, scale=nhomlc[:, h:h + 1], bias=homlc[:, h:h + 1]),
                      r=["A1", "lbk"], w=["A2"])
                    yield
                    A(lambda e: e.activation(out=A1[:, 0:W], in_=A1[:, 0:W], func=AF.Ln, scale=homlc[:, h:h + 1], bias=lbhc[:, h:h + 1]),
                      r=["A1", "A2", "lbk"], w=["A1"])
                    yield
                    V(lambda e: e.tensor_tensor_scan(out=A3[:, 0:W], data0=msk64[:, 0:W], data1=A1[:, 0:W], initial=0.0,
                                                     op0=ALU.mult, op1=ALU.add), r=["A1", "msk64"], w=["A3"])
                    yield
                    A(lambda e: e.activation(out=A4[:, 0:W], in_=A3[:, 0:W], func=AF.Exp), r=["A3"], w=["A4"])
                    yield
                    G(lambda e: e.tensor_tensor(out=QT[:, 0:W], in0=QS[:, 0:W], in1=A4[:, 0:W], op=ALU.mult), r=["QS", "A4"], w=["QT" + sfx])
                    A(lambda e: e.activation(func=AF.Copy, out=DECH[:, 0:16].unsqueeze(2),
                                             in_=A4[:, 0:1024].rearrange("p (c t) -> p c t", t=64)[:, :, 63:64]), r=["A4"], w=["DECH" + sfx])
                    if has_s:
                        A(lambda e: e.activation(func=AF.Copy, out=DECH[:, 16:32].unsqueeze(2),
                                                 in_=A4[:, 1024:1152].rearrange("p (c t) -> p c t", t=8)[:, :, 7:8]),
                          r=["A4", "DECH" + sfx], w=["DECH" + sfx])
                    yield
                    A(lambda e: e.activation(out=A1[:, 0:W], in_=A3[:, 0:W], func=AF.Exp, scale=-1.0), r=["A3", "A1"], w=["A1"])
                    yield
                    G(lambda e: e.tensor_tensor(out=KHAT[:, 0:W], in0=A2[:, 0:W], in1=A1[:, 0:W], op=ALU.mult), r=["A2", "A1"], w=["KHAT" + sfx])
                    yield
                    G(lambda e: e.tensor_tensor(out=A1[:, 0:1024].rearrange("p (c t) -> p c t", t=64),
                                                in0=A1[:, 0:1024].rearrange("p (c t) -> p c t", t=64),
                                                in1=DECH[:, 0:16].unsqueeze(2).to_broadcast([128, 16, 64]), op=ALU.mult),
                      r=["A1", "DECH" + sfx, "KHAT" + sfx], w=["A1"])
                    if has_s:
                        G(lambda e: e.tensor_tensor(out=A1[:, 1024:1152].rearrange("p (c t) -> p c t", t=8),
                                                    in0=A1[:, 1024:1152].rearrange("p (c t) -> p c t", t=8),
                                                    in1=DECH[:, 16:32].unsqueeze(2).to_broadcast([128, 16, 8]), op=ALU.mult),
                          r=["A1", "DECH" + sfx, "KHAT" + sfx], w=["A1"])
                    yield
                    G(lambda e: e.tensor_tensor(out=KTT[:, 0:W], in0=A2[:, 0:W], in1=A1[:, 0:W], op=ALU.mult), r=["A2", "A1"], w=["KTT"])
                    yield
                    b = nb()
                    pinned.add(b)
                    for i in range(8):
                        TR(bbf(b)[:, i * 128:(i + 1) * 128], KTT[:, i * 128:(i + 1) * 128], identb[:], ["KTT", "identb"], [bk(b)], i == 7)
                    A(lambda e: e.activation(func=AF.Copy, out=KA[0:64, :, :], in_=bbf(b)[0:64, :].rearrange("p (i k) -> p i k", i=8)),
                      r=[bk(b)], w=["KA" + sfx])
                    yield
                    A(lambda e: e.activation(func=AF.Copy, out=KB[64:128, :, :], in_=bbf(b)[64:128, :].rearrange("p (i k) -> p i k", i=8)),
                      r=[bk(b)], w=["KB" + sfx])
                    pinned.discard(b)
                    if has_s:
                        b2 = nb()
                        TR(bbf(b2)[:, 0:128], KTT[:, 1024:1152], identb[:], ["KTT", "identb"], [bk(b2)], True)
                        A(lambda e: e.activation(func=AF.Copy, out=KS[:], in_=bbf(b2)[:, 0:128]), r=[bk(b2)], w=["KS" + sfx])
                    yield

                def loop(h):
                    B = HB[h % 2]
                    p_ = h % 2
                    QT, KHAT, KA, KB, DECH, Sbf = B["QT"], B["KHAT"], B["KA"], B["KB"], B["DECH"], B["Sbf"]
                    sfx = "%d" % p_
                    sfk = "SF%d" % h
                    hc = slice(h * 128, (h + 1) * 128)
                    sbk = lambda c_: "Sbf%d_%d" % (p_, c_)
                    cur = 0
                    if ps_ > 0:
                        A(lambda e: e.activation(func=AF.Copy, out=Sbf[0][:], in_=SF[:, h, :]), r=[sfk], w=[sbk(0)])

                    def indep(i):
                        cs = slice(i * 128, (i + 1) * 128)
                        par = i % 2
                        ba = nb()
                        MM(banks[ba][:, 0:128], KHAT[:, cs], QT[:, cs], True, True, ["KHAT" + sfx, "QT" + sfx], [bk(ba)], True)
                        V(lambda e: e.tensor_tensor(out=ATsb[par][:], in0=banks[ba][:, 0:128], in1=mask2[:], op=ALU.mult),
                          r=[bk(ba), "mask2"], w=["ATsb%d" % par])
                        bks = []
                        for KX, kxk in ((KA, "KA" + sfx), (KB, "KB" + sfx)):
                            bc = nb()
                            pinned.add(bc)
                            MM(banks[bc][:, 0:128], KX[:, i, :], vtok[:, i, hc], True, True, [kxk, "vtok_%d" % i], [bk(bc)], True)
                            bks.append(bc)
                        return bks

                    pre = indep(0)
                    for i in range(8):
                        gi = ps_ * 8 + i
                        par = i % 2
                        mine = pre
                        if i < 7:
                            pre = indep(i + 1)
                        bo = nb()
                        first = (gi == 0)
                        for half in range(2):
                            co = slice(half * 128, (half + 1) * 128)
                            qs_ = slice(i * 128 + half * 64, i * 128 + (half + 1) * 64)
                            skip = first and half == 0
                            MM(banks[bo][0:64, co], ATsb[par][:, half * 64:(half + 1) * 64], vtok[:, i, hc], True, skip,
                               ["ATsb%d" % par, "vtok_%d" % i], [bk(bo)], skip and False)
                            if not skip:
                                MM(banks[bo][0:64, co], QT[:, qs_], Sbf[cur][:], False, True, ["QT" + sfx, sbk(cur)], [bk(bo)], half == 1)
                            bc = mine[half]
                            nxt = 1 - cur
                            dcol = 2 * i + half
                            V(lambda e: e.scalar_tensor_tensor(out=Sbf[nxt][:], in0=SF[:, h, :], scalar=DECH[:, dcol:dcol + 1],
                                                               in1=banks[bc][:, 0:128], op0=ALU.mult, op1=ALU.add),
                              r=[sfk, "DECH" + sfx, bk(bc)], w=[sbk(nxt)])
                            V(lambda e: e.scalar_tensor_tensor(out=SF[:, h, :], in0=SF[:, h, :], scalar=DECH[:, dcol:dcol + 1],
                                                               in1=banks[bc][:, 0:128], op0=ALU.mult, op1=ALU.add),
                              r=[sfk, "DECH" + sfx, bk(bc)], w=[sfk])
                            pinned.discard(bc)
                            cur = nxt
                        A(lambda e: e.activation(func=AF.Copy, out=Ohs[p_][:, 2 * i:2 * i + 2, :],
                                                 in_=banks[bo][0:64, 0:256].rearrange("p (c v) -> p c v", c=2)), r=[bk(bo)], w=["Oh%d" % p_])
                        for half in range(2):
                            A(lambda e: e.activation(func=AF.Square, out=SQ[:, (2 * i + half) * 128:(2 * i + half + 1) * 128],
                                                     in_=Ohs[p_][:, 2 * i + half, :],
                                                     accum_out=ssq[p_][:, 2 * i + half:2 * i + half + 1]),
                              r=["Oh%d" % p_], w=["SQ_%d" % (2 * i + half), "ssq%d" % p_])
                        yield

                def postg(h):
                    B = HB[h % 2]
                    p_ = h % 2
                    ZS = B["ZS"]
                    Oh = Ohs[p_]
                    ohk = "Oh%d" % p_
                    sfx = "%d" % p_
                    A(lambda e: e.activation(out=ssh[0:64, 0:16], in_=ssq[p_][:, :], func=AF.Ln, scale=1.0 / 128.0, bias=epsc[0:64, 0:1]),
                      r=["ssq%d" % p_, "epsc"], w=["ssh"])
                    A(lambda e: e.activation(out=ssh[0:64, 0:16], in_=ssh[0:64, 0:16], func=AF.Exp, scale=-0.5), r=["ssh"], w=["ssh"])
                    yield
                    G(lambda e: e.tensor_tensor(out=On, in0=Oh[:], in1=ssh[0:64, 0:16].unsqueeze(2).to_broadcast([64, 16, 128]), op=ALU.mult),
                      r=[ohk, "ssh"], w=["On"])
                    yield
                    b = nb()
                    pinned.add(b)
                    for c in range(16):
                        TR(bbf(b)[:, c * 64:(c + 1) * 64], On[0:64, c, :], identb[0:64, 0:64], ["On", "identb"], [bk(b)], c == 15)
                    yield
                    V(lambda e: e.scalar_tensor_tensor(out=mixB[:, h, 0:1024], in0=bbf(b)[:, 0:1024], scalar=hncol[:, h:h + 1],
                                                       in1=ZS[:, 0:1024], op0=ALU.mult, op1=ALU.mult),
                      r=[bk(b), "consts", "ZS" + sfx], w=["mixB_%d" % h])
                    pinned.discard(b)
                    yield

                def post_sample(h):
                    B = HB[h % 2]
                    p_ = h % 2
                    QT, KHAT, ZS, DECH, KS = B["QT"], B["KHAT"], B["ZS"], B["DECH"], B["KS"]
                    sfx = "%d" % p_
                    sfk = "SF%d" % h
                    hc = slice(h * 128, (h + 1) * 128)
                    P.dma("sp", lambda e: e.dma_start(out=pS_d[h], in_=SF[:, h, :]), "pS", reads=[sfk])
                    cs = slice(1024, 1152)
                    ba = nb()
                    MM(banks[ba][:, 0:128], KHAT[:, cs], QT[:, cs], True, True, ["KHAT" + sfx, "QT" + sfx], [bk(ba)], True)
                    V(lambda e: e.tensor_tensor(out=ATsb[0][:], in0=banks[ba][:, 0:128], in1=bmask[:], op=ALU.mult),
                      r=[bk(ba), "bmask"], w=["ATsb0"])
                    if h + 1 < 8:
                        issue_sst(h + 1)
                    Sst = Ssts[h % 2]
                    sstk = "Sst%d" % (h % 2)
                    A(lambda e: e.activation(func=AF.Copy, out=Sb16[:], in_=Sst[:]), r=[sstk], w=["Sb16"])
                    V(lambda e: e.tensor_copy(out=QpH[:].rearrange("p (b c) -> p b c", c=136)[:, :, 0:8],
                                              in_=QT[:, 1024:1152].rearrange("p (b j) -> p b j", j=8)), r=["QT" + sfx, "QpH"], w=["QpH"])
                    bo = RES
                    MM(banks[bo][:, 0:128], ATsb[0][:], vtok[:, 8, hc], True, False, ["ATsb0", "vtok_8"], [bk(bo)], False)
                    for b_ in range(16):
                        MM(banks[bo][:, 0:128], QpH[:, b_ * 128:(b_ + 1) * 128], Sb16[:, b_, :], False, b_ == 15, ["QpH", "Sb16"], [bk(bo)], b_ == 15)
                    V(lambda e: e.tensor_tensor(out=Kp16[:], in0=KS[:, :].unsqueeze(1).to_broadcast([128, 16, 128]),
                                                in1=bm16[:, :].unsqueeze(2).to_broadcast([128, 16, 128]), op=ALU.mult),
                      r=["KS" + sfx, "bm16"], w=["Kp16"])
                    for bq in range(4):
                        bc = nb()
                        for j in range(4):
                            MM(banks[bc][:, j * 128:(j + 1) * 128], Kp16[:, 4 * bq + j, :], vtok[:, 8, hc], True, True, ["Kp16", "vtok_8"],
                               [bk(bc)], j == 3)
                        G(lambda e: e.tensor_tensor(out=tmpS[:], in0=Sst[:, 4 * bq:4 * bq + 4, :],
                                                    in1=DECH[:, 16 + 4 * bq:20 + 4 * bq].unsqueeze(2).to_broadcast([128, 4, 128]),
                                                    op=ALU.mult), r=[sstk, "Sb16", "DECH" + sfx], w=["tmpS"])
                        V(lambda e: e.tensor_tensor(out=Sst[:, 4 * bq:4 * bq + 4, :], in0=tmpS[:],
                                                    in1=banks[bc][:, :].rearrange("p (j v) -> p j v", j=4), op=ALU.add),
                          r=["tmpS", bk(bc), "Sb16"], w=[sstk])
                    P.dma("sp", lambda e: e.dma_start(out=sSo_d[:, h].rearrange("b k v -> k b v"), in_=Sst[:]), "SstO%d" % (h % 2), reads=[sstk])
                    A(lambda e: e.activation(out=Ons[:], in_=banks[bo][:, 0:128], func=AF.Square, accum_out=ssh[:, 16:17]),
                      r=[bk(bo)], w=["Ons", "ssh2"])
                    A(lambda e: e.activation(out=ssh[:, 17:18], in_=ssh[:, 16:17], func=AF.Ln, scale=1.0 / 128.0, bias=epsc[:, 0:1]),
                      r=["ssh2", "epsc"], w=["ssh3"])
                    A(lambda e: e.activation(out=ssh[:, 17:18], in_=ssh[:, 17:18], func=AF.Exp, scale=-0.5), r=["ssh3"], w=["ssh3"])
                    V(lambda e: e.tensor_scalar(out=Ons[:], in0=banks[bo][:, 0:128], scalar1=ssh[:, 17:18], scalar2=None, op0=ALU.mult),
                      r=[bk(bo), "ssh3", "Ons"], w=["Ons"])
                    b = nb()
                    TR(bbf(b)[:, 0:128], Ons[:], identb[:], ["Ons", "identb"], [bk(b)], True)
                    V(lambda e: e.scalar_tensor_tensor(out=mixB[:, h, 1024:1152], in0=bbf(b)[:, 0:128], scalar=hncol[:, h:h + 1],
                                                       in1=ZS[:, 1024:1152], op0=ALU.mult, op1=ALU.mult),
                      r=[bk(b), "consts", "ZS" + sfx], w=["mixB_%d" % h])

                nrot[0] = 7
                for _ in chain(0):
                    pass
                gp = None
                for h in range(8):
                    gc = chain(h + 1) if h < 7 else None
                    for _ in loop(h):
                        if gp is not None:
                            for _k in range(2):
                                if next(gp, "done") == "done":
                                    gp = None
                                    break
                        elif gc is not None:
                            for _k in range(2):
                                if next(gc, "done") == "done":
                                    gc = None
                                    break
                    if gp is not None:
                        for _ in gp:
                            pass
                    if gc is not None:
                        for _ in gc:
                            pass
                    if has_s:
                        post_sample(h)
                    gp = postg(h)
                for _ in gp:
                    pass
                nrot[0] = 6
                P.full_barrier()

            ck("p%d_ph3" % ps_)
            with ExitStack() as s4:
                wout = sbt(s4, "wout", [128, 16, 1024], BF16)
                gfin = sbt(s4, "gfin", [128, 1024], F32)
                P.dma("sp", lambda e: e.dma_start(out=gfin[:], in_=gfin_d), "gfin", writes=["gfin"])
                xr = [sbt(s4, "xr%d" % i, [128, 1024], F32) for i in range(3)]
                yt = [sbt(s4, "yt%d" % i, [128, 1024], F32) for i in range(2)]
                junk4 = sbt(s4, "junk4", [128, 1024], BF16)
                ss4 = sbt(s4, "ss4", [128, 9], F32)
                for q in range(8):
                    P.dma("pool", lambda e, q=q: e.dma_start(out=wout[:, 2 * q:2 * q + 2, :],
                                                             in_=w_out_d[q * 256:(q + 1) * 256, :].rearrange("(k p) n -> p k n", p=128)),
                          "wout%d" % q, writes=["wout%d" % q])
                mix_keys = ["mixA_%d" % h for h in range(4)] + ["mixB_%d" % h for h in range(8)]
                for i in range(NTp):
                    sl = i % 2
                    xs_ = i % 3
                    P.dma("sp", lambda e, i=i, xs_=xs_: e.dma_start(out=xr[xs_][:], in_=x_d[gt(i)]), "xr%d" % xs_, writes=["xr%d" % xs_])
                    bs = [nb(), nb()]
                    for hf in range(2):
                        for kc in range(16):
                            src = mixA if kc < 8 else mixB
                            MM(banks[bs[hf]][:, :], src[:, kc % 8, i * 128:(i + 1) * 128], wout[:, kc, hf * 512:(hf + 1) * 512], kc == 0, kc == 15,
                               mix_keys + ["wout%d" % (kc // 2)], [bk(bs[hf])], kc == 15)
                        V(lambda e, hf=hf, sl=sl, b=bs[hf]: e.tensor_tensor(out=yt[sl][:, hf * 512:(hf + 1) * 512], in0=banks[b][:, :],
                                                                            in1=xr[xs_][:, hf * 512:(hf + 1) * 512], op=ALU.add),
                          r=[bk(bs[hf]), "xr%d" % xs_], w=["yt%d_%d" % (sl, hf)])
                    A(lambda e, i=i, sl=sl: e.activation(out=junk4[:], in_=yt[sl][:], func=AF.Square, accum_out=ss4[:, i:i + 1]),
                      r=["yt%d_0" % sl, "yt%d_1" % sl], w=["junk4", "ss4_%d" % i])
                    A(lambda e, i=i: e.activation(out=ss4[:, i:i + 1], in_=ss4[:, i:i + 1], func=AF.Ln, scale=1.0 / 1024.0, bias=epsc[:, 0:1]),
                      r=["ss4_%d" % i, "epsc"], w=["ss4_%d" % i])
                    A(lambda e, i=i: e.activation(out=ss4[:, i:i + 1], in_=ss4[:, i:i + 1], func=AF.Exp, scale=-0.5),
                      r=["ss4_%d" % i], w=["ss4_%d" % i])
                    V(lambda e, i=i, sl=sl: e.scalar_tensor_tensor(out=yt[sl][:], in0=yt[sl][:], scalar=ss4[:, i:i + 1], in1=gfin[:],
                                                                   op0=ALU.mult, op1=ALU.mult),
                      r=["yt%d_0" % sl, "yt%d_1" % sl, "ss4_%d" % i, "gfin"], w=["yt%d_0" % sl, "yt%d_1" % sl])
                    P.dma("pool", lambda e, i=i, sl=sl: e.dma_start(out=y_d[gt(i)], in_=yt[sl][:]), "yo%d" % sl,
                          reads=["yt%d_0" % sl, "yt%d_1" % sl])
                P.full_barrier()
                ck("p%d_ph4" % ps_)
                P.final_wait("sp") if ps_ == 1 else None
            s34.close()

        if DBG_PRINT:
            print("SBUF peak bytes/partition:", peak[0], [(n, b) for n, b in peak_names[:40]])
        with nc.Block() as block:
            P.emit(block)
    return nc


_NC_CACHE = {}


def kernel(x_prompt, x_sample, state_mlstm_conv, state_mlstm_C, state_mlstm_n, state_mlstm_m,
           state_hgrn_S, g_norm, w_in, conv_w, conv_b, w_q, w_k, w_v, w_gate, b_gate, m_ln,
           m_skip, lb_param, h_norm, w_out, g_final):
    f = lambda a: np.ascontiguousarray(np.asarray(a, dtype=np.float32))
    x_prompt, x_sample = f(x_prompt), f(x_sample)
    col = lambda v: f(np.asarray(v, np.float32).reshape(8, 128).T)
    shared = {
        "gcol": col(g_norm[0]),
        "w_in": f(w_in[0]),
        "cwcol": f(np.asarray(conv_w[0], np.float32).reshape(4, 8, 128).transpose(2, 1, 0)),
        "cbcol": col(conv_b[0]),
        "w_q": f(w_q[0]), "w_k": f(w_k[0]), "w_v": f(w_v[0]),
        "wqT": f(np.asarray(w_q[0], np.float32).transpose(2, 0, 1)),
        "wkT": f(np.asarray(w_k[0], np.float32).transpose(2, 0, 1)),
        "wvT": f(np.asarray(w_v[0], np.float32).transpose(0, 2, 1).reshape(4, 2, 128, 256).transpose(2, 0, 1, 3)),
        "wg": f(np.asarray(w_gate[0], np.float32).reshape(16, 128, 8).transpose(1, 0, 2)),
        "bg": f(np.asarray(b_gate[0], np.float32).reshape(2, 4).T),
        "mlncol": col(m_ln[0]),
        "mskipcol": col(m_skip[0]),
        "lbp": f(np.asarray(lb_param, np.float32).reshape(2, 8, 128).transpose(2, 0, 1)),
        "hnormcol": col(h_norm[0]),
        "w_out": f(w_out[0]),
        "gfin": f(np.broadcast_to(np.asarray(g_final, np.float32)[None, :], (128, 1024))),
    }
    in_maps = []
    for c in range(8):
        sl = slice(16 * c, 16 * c + 16)
        xa = np.concatenate([x_prompt[c].reshape(16, 128, 1024), x_sample[sl].reshape(1, 128, 1024)], 0)
        m = dict(shared)
        m["x"] = f(xa)
        m["sconv"] = f(np.asarray(state_mlstm_conv[0, sl], np.float32).reshape(48, 1024))
        m["sC"] = f(state_mlstm_C[0, sl])
        m["snT"] = f(np.asarray(state_mlstm_n[0, sl], np.float32).transpose(2, 0, 1))
        m["m0T"] = f(np.asarray(state_mlstm_m[0, sl], np.float32).T)
        m["sS"] = f(state_hgrn_S[0, sl])
        in_maps.append(m)
    if "nc" not in _NC_CACHE:
        _NC_CACHE["nc"] = build_program()
    nc = _NC_CACHE["nc"]
    res = run_bass_kernel_spmd(nc, in_maps, core_ids=list(range(8)))
    R = res.results
    y_prompt = np.stack([R[c]["y"][0:16].reshape(2048, 1024) for c in range(8)], 0)
    y_sample = np.concatenate([R[c]["y"][16].reshape(16, 8, 1024) for c in range(8)], 0)
    p_conv = np.stack([R[c]["pconv"] for c in range(8)], 0)[None]
    p_C = np.stack([R[c]["pC"] for c in range(8)], 0)[None]
    p_n = np.stack([R[c]["pnT"].T for c in range(8)], 0)[None]
    p_m = np.stack([R[c]["pmT"][:, 0] for c in range(8)], 0)[None]
    p_S = np.stack([R[c]["pS"] for c in range(8)], 0)[None]
    s_conv = np.concatenate([R[c]["sconvo"] for c in range(8)], 0)[None]
    s_C = np.concatenate([R[c]["sCo"] for c in range(8)], 0)[None]
    s_n = np.concatenate([R[c]["snTo"].transpose(1, 2, 0) for c in range(8)], 0)[None]
    s_m = np.concatenate([R[c]["smTo"].T for c in range(8)], 0)[None]
    s_S = np.concatenate([R[c]["sSo"] for c in range(8)], 0)[None]
    outs = (y_prompt, y_sample, p_conv, p_C, p_n, p_m, p_S, s_conv, s_C, s_n, s_m, s_S)
    return tuple(np.ascontiguousarray(o, dtype=np.float32) for o in outs)
```
